# Optimizing a Trainium2 kernel written in Bass

```python
import math
import jax, jax.numpy as jnp
from jax import lax
import numpy as np

D_MODEL = 1024
BATCH = 1
SEQ = 16384
DEPTH = 2

HEAD_DIM = 64
ROPE_THETA = 10000.0
CONV_WIDTH = 512
DIFF_HEADS = 4
DIFF_V_DIM = 2 * HEAD_DIM
SWA_Q_HEADS = 8
SWA_KV_HEADS = 2
SWA_WINDOW = 128
SWA_BLOCK = 128
Q_BLOCK = 128
N_BRANCH = 3
BRANCH_WIDTH = 512
D_FF = 2816
LN_EPS = 1e-5
ALPHA = (2 * DEPTH) ** 0.25
BETA = (8 * DEPTH) ** -0.25

DIFF_QK = DIFF_HEADS * 2 * HEAD_DIM
DIFF_V = DIFF_HEADS * DIFF_V_DIM
SWA_Q = SWA_Q_HEADS * HEAD_DIM
SWA_KV = SWA_KV_HEADS * HEAD_DIM
SPLIT_SIZES = (CONV_WIDTH, CONV_WIDTH, CONV_WIDTH, DIFF_QK, DIFF_QK, DIFF_V, SWA_Q, SWA_KV, SWA_KV)
D_IN = 3 * CONV_WIDTH + 2 * DIFF_QK + DIFF_V + SWA_Q + 2 * SWA_KV
VALUE_SEGMENTS = (0, 5, 8)

kernel_name = "hybrid_gated_conv_diffattn_swa_encoder"


def layer_norm(x, g, b):
    xf = x.astype(jnp.float32)
    mu = jnp.mean(xf, axis=-1, keepdims=True)
    var = jnp.mean(jnp.square(xf - mu), axis=-1, keepdims=True)
    y = (xf - mu) * lax.rsqrt(var + LN_EPS)
    return (y * g.astype(jnp.float32) + b.astype(jnp.float32)).astype(x.dtype)


def rms_norm(x, g):
    xf = x.astype(jnp.float32)
    y = xf * lax.rsqrt(jnp.mean(jnp.square(xf), axis=-1, keepdims=True) + LN_EPS)
    return (y * g.astype(jnp.float32)).astype(x.dtype)


def dwconv3(x, w):
    xp = jnp.pad(x, ((0, 0), (1, 1), (0, 0)))
    return w[0] * xp[:, :-2] + w[1] * xp[:, 1:-1] + w[2] * xp[:, 2:]


def rope_tables(positions):
    inv_freq = 1.0 / (ROPE_THETA ** (jnp.arange(0, HEAD_DIM, 2, dtype=jnp.float32) / HEAD_DIM))
    ang = positions.astype(jnp.float32)[..., None] * inv_freq
    return jnp.cos(ang), jnp.sin(ang)


def apply_rope(x, cos, sin):
    shape = cos.shape[:2] + (1,) * (x.ndim - 3) + cos.shape[-1:]
    c = cos.reshape(shape).astype(x.dtype)
    s = sin.reshape(shape).astype(x.dtype)
    x1, x2 = jnp.split(x, 2, axis=-1)
    return jnp.concatenate([x1 * c - x2 * s, x2 * c + x1 * s], axis=-1)


def short_conv_mixer(xt, gate_b, gate_c, w):
    return gate_b * dwconv3(gate_c * xt, w)


def diff_attention(q, k, v, lam, subln_g, lambda_init):
    b, s, h = q.shape[:3]
    nb = s // Q_BLOCK
    scale = HEAD_DIM ** -0.5
    qb = q.reshape(b, nb, Q_BLOCK, h, 2, HEAD_DIM).transpose(1, 0, 2, 3, 4, 5)

    def block(qi):
        sc = jnp.einsum('bqhjd,bkhjd->bhjqk', qi, k).astype(jnp.float32) * scale
        p = jax.nn.softmax(sc, axis=-1)
        a = p[:, :, 0] - lam * p[:, :, 1]
        return jnp.einsum('bhqk,bkhe->bqhe', a.astype(v.dtype), v)

    o = lax.map(block, qb)
    o = o.transpose(1, 0, 2, 3, 4).reshape(b, s, h, DIFF_V_DIM)
    o = rms_norm(o, subln_g) * (1.0 - lambda_init)
    return o.reshape(b, s, h * DIFF_V_DIM)


def window_attention(q, k, v, sink):
    b, s = q.shape[:2]
    nb = s // SWA_BLOCK
    g = SWA_Q_HEADS // SWA_KV_HEADS
    qb = q.reshape(b, nb, SWA_BLOCK, SWA_KV_HEADS, g, HEAD_DIM)

    def band(t):
        tp = jnp.pad(t, ((0, 0), (SWA_BLOCK, SWA_BLOCK), (0, 0), (0, 0)))
        tp = tp.reshape(b, nb + 2, SWA_BLOCK, SWA_KV_HEADS, HEAD_DIM)
        return jnp.concatenate([tp[:, :-2], tp[:, 1:-1], tp[:, 2:]], axis=2)

    kb, vb = band(k), band(v)
    sc = jnp.einsum('bnqhgd,bnkhd->bnhgqk', qb, kb).astype(jnp.float32) * (HEAD_DIM ** -0.5)
    qi = jnp.arange(SWA_BLOCK)[:, None]
    kj = jnp.arange(3 * SWA_BLOCK)[None, :]
    in_window = jnp.abs(kj - SWA_BLOCK - qi) <= SWA_WINDOW
    key_pos = (jnp.arange(nb) * SWA_BLOCK)[:, None, None] - SWA_BLOCK + kj[None]
    valid = in_window[None] & (key_pos >= 0) & (key_pos < s)
    sc = jnp.where(valid[None, :, None, None], sc, -jnp.inf)
    sink_l = sink.astype(jnp.float32).reshape(SWA_KV_HEADS, g)[None, None, :, :, None, None]
    m = jnp.maximum(jnp.max(sc, axis=-1, keepdims=True), sink_l)
    e = jnp.exp(sc - m)
    p = e / (jnp.sum(e, axis=-1, keepdims=True) + jnp.exp(sink_l - m))
    o = jnp.einsum('bnhgqk,bnkhd->bnqhgd', p.astype(v.dtype), vb)
    return o.reshape(b, s, SWA_Q_HEADS * HEAD_DIM)


def conv_ffn(x, w_up, conv_w, w_down):
    hdn = dwconv3(x @ w_up, conv_w)
    gate, up = jnp.split(hdn, 2, axis=-1)
    return (jax.nn.gelu(gate, approximate=False) * up) @ w_down


def setup_inputs(seed: int = 0) -> dict:
    key = jax.random.key(seed)
    ks = jax.random.split(key, 24)
    f32 = jnp.float32
    nrm = lambda k, shape, sc: jax.random.normal(k, shape, f32) * sc

    col_scale = np.concatenate([np.full((n,), BETA if i in VALUE_SEGMENTS else 1.0, np.float32)
                                for i, n in enumerate(SPLIT_SIZES)])
    x = jax.random.normal(ks[0], (BATCH, SEQ, D_MODEL), f32)
    positions = jnp.broadcast_to(jnp.arange(SEQ, dtype=jnp.int32), (BATCH, SEQ))
    return {
        "x": x,
        "positions": positions,
        "ln_in_g": 1.0 + nrm(ks[1], (D_MODEL,), 0.02),
        "ln_in_b": nrm(ks[2], (D_MODEL,), 0.02),
        "w_in": nrm(ks[3], (DEPTH, D_MODEL, D_IN), D_MODEL ** -0.5) * jnp.asarray(col_scale),
        "conv_w": nrm(ks[4], (DEPTH, 3, CONV_WIDTH), 3 ** -0.5),
        "diff_lambda": nrm(ks[5], (DEPTH, 4, HEAD_DIM), 0.1),
        "diff_subln_g": 1.0 + nrm(ks[6], (DEPTH, DIFF_V_DIM), 0.02),
        "swa_sink": nrm(ks[7], (DEPTH, SWA_Q_HEADS), 0.5),
        "w_branch_gate": nrm(ks[8], (DEPTH, D_MODEL, N_BRANCH * D_MODEL), D_MODEL ** -0.5),
        "b_branch_gate": nrm(ks[9], (DEPTH, N_BRANCH * D_MODEL), 0.02),
        "w_branch": nrm(ks[10], (DEPTH, N_BRANCH, BRANCH_WIDTH, D_MODEL), BRANCH_WIDTH ** -0.5 * BETA),
        "w_o": nrm(ks[11], (DEPTH, D_MODEL, D_MODEL), D_MODEL ** -0.5 * BETA),
        "ln_mix_g": 1.0 + nrm(ks[12], (DEPTH, D_MODEL), 0.02),
        "ln_mix_b": nrm(ks[13], (DEPTH, D_MODEL), 0.02),
        "w_ffn_up": nrm(ks[14], (DEPTH, D_MODEL, 2 * D_FF), D_MODEL ** -0.5 * BETA),
        "ffn_conv_w": nrm(ks[15], (DEPTH, 3, 2 * D_FF), 3 ** -0.5),
        "w_ffn_down": nrm(ks[16], (DEPTH, D_FF, D_MODEL), D_FF ** -0.5 * BETA),
        "ln_ffn_g": 1.0 + nrm(ks[17], (DEPTH, D_MODEL), 0.02),
        "ln_ffn_b": nrm(ks[18], (DEPTH, D_MODEL), 0.02),
    }


def reference(x, positions, ln_in_g, ln_in_b, w_in, conv_w, diff_lambda, diff_subln_g, swa_sink,
              w_branch_gate, b_branch_gate, w_branch, w_o, ln_mix_g, ln_mix_b,
              w_ffn_up, ffn_conv_w, w_ffn_down, ln_ffn_g, ln_ffn_b):
    b, s, _ = x.shape
    split_at = np.cumsum(SPLIT_SIZES)[:-1].tolist()
    cos, sin = rope_tables(positions)
    h = layer_norm(x, ln_in_g, ln_in_b)
    for l in range(DEPTH):
        lambda_init = 0.8 - 0.6 * math.exp(-0.3 * l)
        proj = h @ w_in[l]
        a_x, a_b, a_c, dq, dk, dv, sq, sk, sv = jnp.split(proj, split_at, axis=-1)

        y_a = short_conv_mixer(a_x, a_b, a_c, conv_w[l])

        dq = apply_rope(dq.reshape(b, s, DIFF_HEADS, 2, HEAD_DIM), cos, sin)
        dk = apply_rope(dk.reshape(b, s, DIFF_HEADS, 2, HEAD_DIM), cos, sin)
        dv = dv.reshape(b, s, DIFF_HEADS, DIFF_V_DIM)
        lp = diff_lambda[l].astype(jnp.float32)
        lam = jnp.exp(jnp.sum(lp[0] * lp[1])) - jnp.exp(jnp.sum(lp[2] * lp[3])) + lambda_init
        y_b = diff_attention(dq, dk, dv, lam, diff_subln_g[l], lambda_init)

        sq = apply_rope(sq.reshape(b, s, SWA_Q_HEADS, HEAD_DIM), cos, sin)
        sk = apply_rope(sk.reshape(b, s, SWA_KV_HEADS, HEAD_DIM), cos, sin)
        sv = sv.reshape(b, s, SWA_KV_HEADS, HEAD_DIM)
        y_c = window_attention(sq, sk, sv, swa_sink[l])

        ys = jnp.stack([y_a, y_b, y_c], axis=2)
        branches = jnp.einsum('bsnc,ncd->bsnd', ys, w_branch[l])
        gates = jax.nn.sigmoid(h @ w_branch_gate[l] + b_branch_gate[l]).reshape(b, s, N_BRANCH, D_MODEL)
        mix = jnp.sum(gates * branches, axis=2) @ w_o[l]
        h = layer_norm(ALPHA * h + mix, ln_mix_g[l], ln_mix_b[l])

        f = conv_ffn(h, w_ffn_up[l], ffn_conv_w[l], w_ffn_down[l])
        h = layer_norm(ALPHA * h + f, ln_ffn_g[l], ln_ffn_b[l])
    return h
```

```python
import math
from contextlib import ExitStack

import numpy as np
import concourse.bass as bass
import concourse.mybir as mybir
from concourse.bass_utils import run_bass_kernel_spmd

F32 = mybir.dt.float32
BF16 = mybir.dt.bfloat16
I32 = mybir.dt.int32
AF = mybir.ActivationFunctionType
ALU = mybir.AluOpType
AX = mybir.AxisListType

NCORES = 8
SEQ = 16384
T = SEQ // NCORES
NT = T // 128
D = 1024
DEPTH = 2
D_IN = 3840
D_FF = 2816
NFF = D_FF // 128
LN_EPS = 1e-5
ALPHA = (2 * DEPTH) ** 0.25
THETA = 10000.0
MAGIC = 12582912.0
TWO_PI = 2.0 * math.pi
C1 = 6.28125
C2 = float(np.float32(TWO_PI - C1))

BIGW = [("w_in", 1024, 3840), ("w_branch_gate", 1024, 3072), ("w_branch", 1536, 1024), ("w_o", 1024, 1024),
        ("w_ffn_up", 1024, 5632), ("w_ffn_down", 2816, 1024)]
O_AX, O_AB, O_AC, O_DQ, O_DK, O_DV, O_SQ, O_SK, O_SV = 0, 512, 1024, 1536, 2048, 2560, 3072, 3584, 3712


class Ev:
    def __init__(self, nc, es, name):
        self.sem = es.enter_context(nc.semaphore(name))
        self.n = 0

    def s(self, ins, k=1):
        ins.then_inc(self.sem, k)
        self.n += k
        return self.n

    def w(self, eng, v=None):
        v = self.n if v is None else v
        if v > 0:
            eng.wait_ge(self.sem, v)


def build_nc(dbg=None):
    nc = bass.Bass("TRN2", target_bir_lowering=False)
    ES = ExitStack()
    PE, ACT, DVE, POOL, SP = nc.tensor, nc.scalar, nc.vector, nc.gpsimd, nc.sync
    ENGS = [PE, ACT, DVE, POOL, SP]
    evc = [0]

    def ev(name):
        evc[0] += 1
        return Ev(nc, ES, f"{name}_{evc[0]}")

    def din(name, shape, dt=F32):
        return nc.dram_tensor(name, list(shape), dt, kind="ExternalInput")

    x_d = din("x", [T, D])
    pos_d = din("pos", [1, T], I32)
    cvec_d = din("cvec", [128, 4])
    selv_d = din("selv", [128, 16])
    sel16_d = din("sel16", [16, 2])
    ident_d = din("ident", [128, 128])
    mask_d = din("masks", [128, 2, 512])
    lnin_g_d = din("ln_in_g", [1, D])
    lnin_b_d = din("ln_in_b", [1, D])
    conv_w_d = din("conv_w", [DEPTH, 3, 512])
    dlam_d = din("diff_lambda", [DEPTH, 256])
    subg_d = din("diff_subln_g", [DEPTH, 128])
    sink_d = din("swa_sink", [DEPTH, 8])
    bg_d = din("b_branch_gate", [DEPTH, 3 * D])
    lnm_g_d = din("ln_mix_g", [DEPTH, D])
    lnm_b_d = din("ln_mix_b", [DEPTH, D])
    fcw_d = din("ffn_conv_w", [DEPTH, 3, 2 * D_FF])
    lnf_g_d = din("ln_ffn_g", [DEPTH, D])
    lnf_b_d = din("ln_ffn_b", [DEPTH, D])
    out_d = nc.dram_tensor("out", [T, D], F32, kind="ExternalOutput")
    dbg_d = None
    if dbg is not None:
        dbg_d = nc.dram_tensor("dbg", list(dbg[1]), F32 if len(dbg) < 3 else dbg[2], kind="ExternalOutput")

    hres_d = nc.dram_tensor("hres", [T, D], F32)
    WF = {}
    for l in range(DEPTH):
        for (nm, R, C) in BIGW:
            sh = din(f"{nm}{l}", [R // NCORES, C])
            bb = nc.dram_tensor(f"{nm}{l}_b", [R // NCORES, C], F32)
            ff = nc.dram_tensor(f"{nm}{l}_f", [R, C], F32, addr_space="Shared")
            WF[(nm, l)] = [sh, bb, ff, None]
    gk_in = [nc.dram_tensor(f"gk_in{l}", [512, T], BF16) for l in range(DEPTH)]
    gk_out = [nc.dram_tensor(f"gk_out{l}", [NCORES * 512, T], BF16, addr_space="Shared") for l in range(DEPTH)]
    gv_in = [nc.dram_tensor(f"gv_in{l}", [512, 16 * 129], BF16) for l in range(DEPTH)]
    gv_out = [nc.dram_tensor(f"gv_out{l}", [NCORES * 512, 16 * 129], BF16, addr_space="Shared") for l in range(DEPTH)]
    ge_in = [nc.dram_tensor(f"ge_in{l}", [128, 516], BF16) for l in range(DEPTH)]
    ge_out = [nc.dram_tensor(f"ge_out{l}", [NCORES * 128, 516], BF16, addr_space="Shared") for l in range(DEPTH)]
    gu_in = [nc.dram_tensor(f"gu_in{l}", [128, 8], F32) for l in range(DEPTH)]
    gu_out = [nc.dram_tensor(f"gu_out{l}", [NCORES * 128, 8], F32, addr_space="Shared") for l in range(DEPTH)]
    e2_in = [nc.dram_tensor(f"e2_in{l}", [2, D], F32) for l in range(DEPTH)]
    e2_out = [nc.dram_tensor(f"e2_out{l}", [NCORES * 2, D], F32, addr_space="Shared") for l in range(DEPTH)]

    sbc = [0]

    def sbuf(st, name, shape, dt):
        sbc[0] += 1
        return st.enter_context(nc.sbuf_tensor(f"s{sbc[0]}_{name}", list(shape), dt))

    hT = sbuf(ES, "hT", [128, 8, T + 2], BF16)
    CC = sbuf(ES, "CC", [128, T], F32)
    TS = sbuf(ES, "TS", [128, T], F32)
    cvec = sbuf(ES, "cvec", [128, 4], F32)
    selv = sbuf(ES, "selv", [128, 16], F32)
    sel16b = sbuf(ES, "sel16b", [16, 2], BF16)
    ident = sbuf(ES, "ident", [128, 128], F32)
    maskb = sbuf(ES, "maskb", [128, 2, 512], BF16)
    mhalf = sbuf(ES, "mhalf", [128, 8], F32)
    halfpi = sbuf(ES, "halfpi", [128, 1], F32)
    epsc = sbuf(ES, "epsc", [128, 1], F32)

    PS = ES.enter_context(nc.psum_tensor("PS", [128, 8, 512], F32))

    bar = ev("bar")

    def barrier():
        for e in ENGS:
            e.drain()
            e.sem_inc(bar.sem, 1)
        bar.n += len(ENGS)
        for e in ENGS:
            bar.w(e)

    gdma = ev("gdma")

    def load_wait(pairs, eng_list):
        for dst, src in pairs:
            gdma.s(SP.dma_start(out=dst, in_=src), 16)
        for e in eng_list:
            gdma.w(e)

    c0 = ev("c0")
    with ExitStack() as st:
        sel16f = sbuf(st, "sel16f", [16, 2], F32)
        maskf = sbuf(st, "maskf", [128, 2, 512], F32)
        load_wait([(cvec[:], cvec_d[:, :]), (selv[:], selv_d[:, :]), (sel16f[:], sel16_d[:, :]),
                   (ident[:], ident_d[:, :]), (maskf[:], mask_d[:, :, :])], [DVE, POOL, ACT, PE])
        DVE.tensor_copy(maskb[:], maskf[:])
        DVE.tensor_copy(sel16b[:], sel16f[:])
        DVE.memset(mhalf[:], -0.5)
        DVE.memset(halfpi[:], math.pi / 2.0)
        c0.s(DVE.memset(epsc[:], LN_EPS))
        for e in ENGS:
            c0.w(e)
        barrier()

    RG = [list(range(NCORES))]
    e1 = ev("wsh")
    for l in range(DEPTH):
        for (nm, R, C) in BIGW:
            sh, bb, ff, _ = WF[(nm, l)]
            e1.s(SP.dma_start(out=bb[:, :], in_=sh[:, :]), 16)
    e1.w(POOL)
    for l in range(DEPTH):
        for (nm, R, C) in BIGW:
            sh, bb, ff, _ = WF[(nm, l)]
            e2 = ev("wcc")
            POOL.collective_compute("AllGather", ALU.bypass, replica_groups=RG,
                                    ins=[bb.ap().opt()], outs=[ff.ap().opt()]).then_inc(e2.sem)
            e2.n = 1
            WF[(nm, l)][3] = e2

    with ExitStack() as st:
        posi = sbuf(st, "posi", [128, T], I32)
        ta = sbuf(st, "ta", [128, T], F32)
        tb = sbuf(st, "tb", [128, T], F32)
        tk = sbuf(st, "tk", [128, T], F32)
        load_wait([(posi[:], pos_d[0:1, :].partition_broadcast(128))], [DVE])
        ch = ev("rt")
        ch.s(DVE.tensor_copy(ta[:], posi[:]))
        ch.w(DVE)
        ch.s(DVE.tensor_scalar(tb[:], ta[:], cvec[:, 0:1], None, ALU.mult))
        ch.w(DVE)
        ch.s(DVE.tensor_scalar(tk[:], tb[:], 1.0 / TWO_PI, MAGIC, ALU.mult, ALU.add))
        ch.w(DVE)
        ch.s(DVE.tensor_scalar(ta[:], tk[:], -MAGIC, None, ALU.add))
        ch.w(DVE)
        ch.s(DVE.scalar_tensor_tensor(tk[:], ta[:], -C1, tb[:], ALU.mult, ALU.add))
        ch.w(DVE)
        ch.s(DVE.scalar_tensor_tensor(tb[:], ta[:], -C2, tk[:], ALU.mult, ALU.add))
        ch.w(DVE)
        ch.s(DVE.tensor_scalar(tk[:], tb[:], -1.0, None, ALU.mult))
        ch.w(DVE)
        ch.s(DVE.tensor_tensor(ta[:], tk[:], tb[:], ALU.max))
        ch.w(ACT)
        ACT.activation(TS[:], tb[:], AF.Sin, scale=cvec[:, 1:2])
        ch.s(ACT.activation(CC[:], ta[:], AF.Sin, bias=halfpi[:, 0:1], scale=-1.0))
        barrier()

    class LNState:
        pass

    L = LNState()
    L.e_stat, L.e_aggr, L.e_p1, L.e_p2, L.e_p3 = ev("lns"), ev("lna"), ev("lnp1"), ev("lnp2"), ev("lnp3")
    L.e_nrm, L.e_mg, L.e_ho, L.e_tr, L.e_cp, L.e_st = ev("lnn"), ev("lnm"), ev("lnh"), ev("lnt"), ev("lnc"), ev("lnst")
    L.cnt = 0

    def ln_setup(st, g_src, b_src):
        L.g = sbuf(st, "ln_g", [128, D], F32)
        L.b = sbuf(st, "ln_b", [128, D], F32)
        L.stats = sbuf(st, "ln_stats", [128, 2, 2, 6], F32)
        L.mv = sbuf(st, "ln_mv", [128, 2, 2], F32)
        L.sm = sbuf(st, "ln_sm", [128, 2, 4], F32)
        L.nrm = sbuf(st, "ln_nrm", [128, 2, D], F32)
        L.ho = sbuf(st, "ln_ho", [128, 2, D], F32)
        load_wait([(L.g[:], g_src.partition_broadcast(128)), (L.b[:], b_src.partition_broadcast(128))], [DVE, POOL])

    def ln_tile(pre, pre_ready, t, dst_dram, write_hT=True, tr_banks=(6, 7)):
        sl = L.cnt % 2
        L.cnt += 1
        pre_ready[0].w(DVE, pre_ready[1])
        L.e_p3.w(DVE, L.e_p3.n - 1)
        DVE.bn_stats(L.stats[:, sl, 0, :], pre[:, 0:512])
        L.e_stat.s(DVE.bn_stats(L.stats[:, sl, 1, :], pre[:, 512:1024]))
        L.e_stat.w(DVE)
        L.e_aggr.s(DVE.bn_aggr(L.mv[:, sl, :], L.stats[:, sl, :, :]))
        L.e_aggr.w(POOL)
        L.e_nrm.w(POOL, L.e_nrm.n - 1)
        L.e_p1.s(POOL.tensor_scalar(L.sm[:, sl, 0:1], L.mv[:, sl, 1:2], LN_EPS, None, ALU.add))
        L.e_p1.w(POOL)
        L.e_p2.s(POOL.tensor_tensor(L.sm[:, sl, 1:2], L.sm[:, sl, 0:1], mhalf[:, 0:1], ALU.pow))
        L.e_p2.w(POOL)
        L.e_p3.s(POOL.tensor_scalar(L.sm[:, sl, 2:3], L.mv[:, sl, 0:1], L.sm[:, sl, 1:2], -1.0, ALU.mult, ALU.mult))
        L.e_p3.w(ACT)
        L.e_ho.w(ACT, L.e_ho.n - 1)
        L.e_nrm.s(ACT.activation(L.nrm[:, sl, :], pre, AF.Identity, bias=L.sm[:, sl, 2:3], scale=L.sm[:, sl, 1:2]))
        L.e_nrm.w(DVE)
        L.e_mg.s(DVE.tensor_tensor(L.nrm[:, sl, :], L.nrm[:, sl, :], L.g[:], ALU.mult))
        L.e_mg.w(POOL)
        L.e_st.w(POOL, L.e_st.n - 16)
        L.e_tr.w(POOL, L.e_tr.n - 1)
        L.e_ho.s(POOL.tensor_tensor(L.ho[:, sl, :], L.nrm[:, sl, :], L.b[:], ALU.add))
        L.e_ho.w(SP)
        L.e_st.s(SP.dma_start(out=dst_dram[t * 128:(t + 1) * 128, :], in_=L.ho[:, sl, :]), 16)
        if write_hT:
            ln_transposes(L.ho[:, sl, :], t, tr_banks)
        return sl

    def ln_transposes(src, t, tr_banks, src_ready=None):
        if src_ready is None:
            L.e_ho.w(PE)
        else:
            src_ready[0].w(PE, src_ready[1])
        if len(tr_banks) == 2:
            L.e_cp.w(PE)
            for c in range(8):
                bk = tr_banks[c // 4]
                ins = PE.transpose(PS[:, bk, (c % 4) * 128:(c % 4 + 1) * 128], src[:, c * 128:(c + 1) * 128], ident[:])
            L.e_tr.s(ins)
            L.e_tr.w(DVE)
            for hb in range(2):
                ins = DVE.tensor_copy(hT[:, hb * 4:(hb + 1) * 4, 1 + t * 128:1 + (t + 1) * 128],
                                      PS[:, tr_banks[hb], :].rearrange("p (c n) -> p c n", c=4))
            L.e_cp.s(ins)
        else:
            bk = tr_banks[0]
            for hb in range(2):
                L.e_cp.w(PE)
                for c4 in range(4):
                    c = hb * 4 + c4
                    ins = PE.transpose(PS[:, bk, c4 * 128:(c4 + 1) * 128], src[:, c * 128:(c + 1) * 128], ident[:])
                L.e_tr.s(ins)
                L.e_tr.w(DVE)
                L.e_cp.s(DVE.tensor_copy(hT[:, hb * 4:(hb + 1) * 4, 1 + t * 128:1 + (t + 1) * 128],
                                         PS[:, bk, :].rearrange("p (c n) -> p c n", c=4)))

    class WS:
        def __init__(self):
            self.dsem = [ev(f"wsd{k}") for k in range(4)]
            self.cast = ev("wscast")
            self.i = 0
            self.pend = []

        def setup(self, st, nstage=3, free=1024, defer=1):
            assert not self.pend
            self.n = nstage
            self.defer = defer
            self.stg = sbuf(st, "wstg", [128, nstage, free], F32)

        def _emit_cast(self):
            (k, need, dst, dst_free, stg) = self.pend.pop(0)
            self.dsem[k].w(ACT, need)
            if dst_free is not None:
                dst_free[0].w(ACT, dst_free[1])
            sh = dst.shape
            sz = 1
            for v in sh[1:]:
                sz *= v
            sview = stg[:, k, 0:sz]
            if len(sh) == 3:
                sview = sview.rearrange("p (a b) -> p a b", a=sh[1])
            self.cast.s(ACT.copy(dst, sview))

        def flush(self):
            while self.pend:
                self._emit_cast()

        def fetch(self, srcs, dst, dst_free=None, gate=None):
            k = self.i % self.n
            self.i += 1
            if gate is not None:
                gate.w(SP, 1)
            if self.i - self.n > 0:
                self.cast.w(SP, self.i - self.n)
            for (src, view) in srcs:
                self.dsem[k].s(SP.dma_start(out=view(self.stg[:, k, :]), in_=src), 16)
            self.pend.append((k, self.dsem[k].n, dst, dst_free, self.stg))
            while len(self.pend) > self.defer:
                self._emit_cast()
            return self.i

    ws = WS()

    def v3(a, b, c0=0, cw=None):
        def f(sl):
            v = sl[:, 0:a * b].rearrange("p (a b) -> p a b", a=a)
            if cw is not None:
                v = v[:, :, c0:c0 + cw]
            return v
        return f

    def dbg_dump(st_list, src_ap, shape, cast=True):
        de = ev("dbg")
        with ExitStack() as s2:
            if cast:
                tmpf = sbuf(s2, "dbgf", shape, F32)
                de.s(DVE.tensor_copy(tmpf[:], src_ap))
                de.w(SP)
                de.s(SP.dma_start(out=dbg_d.ap(), in_=tmpf[:]), 16)
            else:
                de.s(SP.dma_start(out=dbg_d.ap(), in_=src_ap), 16)
            de.w(SP)
        for s_ in st_list:
            s_.close()
        ES.close()
        return nc

    with ExitStack() as st:
        ln_setup(st, lnin_g_d[0:1, :], lnin_b_d[0:1, :])
        xt = sbuf(st, "xt", [128, 2, D], F32)
        xs = [ev("xl0"), ev("xl1")]
        for t in range(NT):
            sl = t % 2
            L.e_nrm.w(SP, L.e_nrm.n - 1)
            xs[sl].s(SP.dma_start(out=xt[:, sl, :], in_=x_d[t * 128:(t + 1) * 128, :]), 16)
            ln_tile(xt[:, sl, :], (xs[sl], xs[sl].n), t, hres_d)
        L.e_st.w(SP)
        barrier()
        if dbg is not None and dbg[0] == "hT0":
            return dbg_dump([st], hT[:], [128, 8, T + 2])

    class PRing:
        def __init__(self, banks):
            self.banks = banks
            self.i = 0
            self.rel = {}

        def acquire(self):
            i = self.i
            self.i += 1
            j = i - len(self.banks)
            if j >= 0:
                for (e_, c_) in self.rel[j]:
                    e_.w(PE, c_)
            return i, self.banks[i % len(self.banks)]

        def release(self, i, *pairs):
            self.rel[i] = list(pairs)

    e_pe = ev("pe")
    e_xs, e_dv, e_pb, e_out = ev("xs"), ev("dv"), ev("pb"), ev("out")
    e_vc = ev("vc")
    e_ac, e_u, e_ue, e_c, e_y = ev("ac"), ev("u"), ev("ue"), ev("c"), ev("y")
    e_prm = ev("prm")
    e_go = ev("go")
    e_ms = ev("ms")
    e_sg, e_pr, e_s01, e_mgd, e_pre, e_fh = ev("sg"), ev("pr"), ev("s01"), ev("mgd"), ev("pre"), ev("fh")
    e_gu, e_gl, e_g, e_hd = ev("gu"), ev("gl"), ev("g"), ev("hd")
    hrs = [ev("hr0"), ev("hr1")]

    def stage_dump(name, stacks, src, shape, cast=True):
        if dbg is not None and dbg[0] == name:
            barrier()
            return dbg_dump(stacks, src, shape, cast)
        return None

    e_fx = ev("fx")
    e_ws, e_wexp, e_wm, e_wpv, e_wo, e_w1, e_wy, e_wtr, e_tcp = (ev("ws"), ev("wexp"), ev("wm"), ev("wpv"), ev("wo"),
                                                                  ev("w1"), ev("wy"), ev("wtr"), ev("tcp"))
    kvs = [ev("kv0"), ev("kv1")]
    e_qk, e_exp, e_pv, e_oev, e_n, e_np, e_yb, e_ytr, e_ycp = (ev("qk"), ev("exp"), ev("pv"), ev("oev"), ev("n"),
                                                                ev("np"), ev("yb"), ev("ytr"), ev("ycp"))
    for l in range(DEPTH):
        lam_init = 0.8 - 0.6 * math.exp(-0.3 * l)
        win_f, win_gate = WF[("w_in", l)][2], WF[("w_in", l)][3]
        LY = ExitStack()
        MIX = ExitStack()
        convw = sbuf(LY, f"convw{l}", [128, 12], F32)
        fcw = sbuf(LY, f"fcw{l}", [128, 132], F32)
        bgt = sbuf(LY, f"bgt{l}", [128, 24], F32)
        neglam = sbuf(LY, f"neglam{l}", [128, 1], F32)
        gsub = sbuf(LY, f"gsub{l}", [128, 128], F32)
        esink = sbuf(LY, f"esink{l}", [128, 8], F32)
        uedge = sbuf(LY, f"uedge{l}", [128, 8], F32)
        abe = sbuf(LY, f"abe{l}", [128, 8], F32)
        c3e = sbuf(LY, f"c3e{l}", [128, 8], F32)
        TMP = ExitStack()
        cwn = sbuf(TMP, f"cwn{l}", [3, 512], F32)
        fcn = sbuf(TMP, f"fcn{l}", [3, 2 * D_FF], F32)
        bgn = sbuf(TMP, f"bgn{l}", [24, 128], F32)
        lamt = sbuf(TMP, f"lamt{l}", [128, 256], F32)
        lamp = sbuf(TMP, f"lamp{l}", [128, 128], F32)
        lams = sbuf(TMP, f"lams{l}", [128, 4], F32)
        load_wait([(cwn[:], conv_w_d[l]), (fcn[:], fcw_d[l]), (bgn[:], bg_d[l].rearrange("(a p) -> a p", p=128)),
                   (lamt[:], dlam_d[l:l + 1, :].partition_broadcast(128)),
                   (gsub[:], subg_d[l:l + 1, :].partition_broadcast(128)),
                   (esink[:], sink_d[l:l + 1, :].partition_broadcast(128))], [PE, DVE, ACT, POOL])
        for cc in range(4):
            PE.transpose(PS[:, 7, cc * 3:(cc + 1) * 3], cwn[0:3, cc * 128:(cc + 1) * 128], ident[0:3, 0:3])
        for m in range(44):
            PE.transpose(PS[:, 7, 12 + m * 3:12 + (m + 1) * 3], fcn[0:3, m * 128:(m + 1) * 128], ident[0:3, 0:3])
        e_prm.s(PE.transpose(PS[:, 7, 144:168], bgn[0:24, :], ident[0:24, 0:24]))
        e_prm.w(DVE)
        DVE.tensor_copy(convw[:], PS[:, 7, 0:12])
        DVE.tensor_copy(fcw[:], PS[:, 7, 12:144])
        DVE.tensor_copy(bgt[:], PS[:, 7, 144:168])
        DVE.tensor_tensor(lamp[:, 0:64], lamt[:, 0:64], lamt[:, 64:128], ALU.mult)
        e_prm.s(DVE.tensor_tensor(lamp[:, 64:128], lamt[:, 128:192], lamt[:, 192:256], ALU.mult))
        e_prm.w(DVE)
        e_prm.s(DVE.tensor_reduce(lams[:, 0:2], lamp[:].rearrange("p (a b) -> p a b", a=2), AX.X, ALU.add))
        e_prm.w(ACT)
        ACT.activation(esink[:], esink[:], AF.Exp)
        e_prm.s(ACT.activation(lams[:, 2:4], lams[:, 0:2], AF.Exp))
        e_prm.w(DVE)
        e_prm.s(DVE.tensor_tensor(neglam[:], lams[:, 3:4], lams[:, 2:3], ALU.subtract))
        e_prm.w(DVE)
        DVE.tensor_scalar(neglam[:], neglam[:], -lam_init, None, ALU.add)
        e_prm.s(DVE.tensor_scalar(gsub[:], gsub[:], 1.0 - lam_init, None, ALU.mult))
        for e in ENGS:
            e_prm.w(e)
        barrier()
        TMP.close()
        yT = sbuf(MIX, f"yT{l}", [128, 12 * T + 512], BF16)
        yaT = yT[:, 0:4 * T].rearrange("p (c t) -> p c t", c=4)
        ybT = yT[:, 4 * T:8 * T].rearrange("p (c t) -> p c t", c=4)
        ycT = yT[:, 8 * T:12 * T].rearrange("p (c t) -> p c t", c=4)
        Kown = ybT
        Vown = yT[:, 8 * T:8 * T + 4 * 16 * 129].rearrange("p (h t e) -> p h t e", h=4, t=16)
        QS = ExitStack()
        QT = sbuf(QS, f"QT{l}", [128, 4, T], BF16)
        WIN = ExitStack()
        SQT = sbuf(WIN, f"SQT{l}", [128, 4, T], BF16)
        SKx = sbuf(WIN, f"SKx{l}", [128, 18 * 128], BF16)
        SVx = sbuf(WIN, f"SVx{l}", [128, 18, 2, 65], BF16)

        PA = ExitStack()
        ws.setup(PA, 3, 1024)
        NWS = 4
        Wsl = sbuf(PA, f"Wsl{l}", [128, NWS, 8, 128], BF16)
        ubuf = sbuf(PA, f"ubuf{l}", [128, T + 2], F32)
        Xs = sbuf(PA, f"Xs{l}", [128, 2, 512], F32)
        Ab = sbuf(PA, f"Ab{l}", [128, 2, 512], F32)
        Bb = sbuf(PA, f"Bb{l}", [128, 2, 512], F32)
        tmpc = sbuf(PA, f"tmpc{l}", [128, 2, 512], F32)
        cb = sbuf(PA, f"cb{l}", [128, 3, 512], F32)
        e_ms.s(POOL.memset(ubuf[:], 0.0))
        e_ms.s(POOL.memset(yT[:, 8 * T:12 * T + 512], 1.0))
        e_ms.s(POOL.memset(SVx[:], 1.0))
        for e in (DVE, ACT, PE):
            e_ms.w(e)

        pr = PRing([0, 1, 2, 3, 4, 5])
        wblk = [0]
        wrel = {}

        def fetch_in(col_segs):
            bi = wblk[0]
            wblk[0] += 1
            slot = bi % NWS
            srcs = []
            for (c0_, ncol, d0) in col_segs:
                srcs.append((win_f[:, c0_:c0_ + ncol].rearrange("(k p) c -> p k c", p=128), v3(8, 128, d0, ncol)))
            dfree = None
            if bi >= NWS:
                dfree = (e_pe, wrel[bi - NWS])
            cnt = ws.fetch(srcs, Wsl[:, slot, :, :], dst_free=dfree, gate=win_gate)
            return bi, slot, cnt

        def rope_block(blk, dst_fn):
            bi, slot, cnt = blk
            for tc in range(4):
                i, bk = pr.acquire()
                if tc == 0:
                    PE.wait_ge(ws.cast.sem, cnt)
                for k in range(8):
                    ins = PE.matmul(PS[:, bk, :], Wsl[:, slot, k, :], hT[:, k, 1 + tc * 512:1 + (tc + 1) * 512],
                                    start=(k == 0), stop=(k == 7))
                e_pe.s(ins)
                sl = e_xs.n % 2
                e_pe.w(ACT)
                e_dv.w(ACT, e_dv.n - 1)
                e_xs.s(ACT.copy(Xs[:, sl, :], PS[:, bk, :]))
                pr.release(i, (e_xs, e_xs.n))
                cs = slice(tc * 512, (tc + 1) * 512)
                e_xs.w(DVE)
                e_out.w(DVE, e_out.n - 1)
                DVE.tensor_tensor(Ab[:, sl, :], Xs[:, sl, :], CC[:, cs], ALU.mult)
                DVE.tensor_tensor(Bb[0:32, sl, :], Xs[32:64, sl, :], TS[32:64, cs], ALU.mult)
                DVE.tensor_tensor(Bb[32:64, sl, :], Xs[0:32, sl, :], TS[0:32, cs], ALU.mult)
                DVE.tensor_tensor(Bb[64:96, sl, :], Xs[96:128, sl, :], TS[96:128, cs], ALU.mult)
                e_dv.s(DVE.tensor_tensor(Bb[96:128, sl, :], Xs[64:96, sl, :], TS[64:96, cs], ALU.mult))
                e_dv.w(POOL)
                e_out.s(POOL.tensor_tensor(dst_fn(tc), Ab[:, sl, :], Bb[:, sl, :], ALU.add))
            wrel[bi] = e_pe.n

        def v_block(blk, dst_fn, view):
            bi, slot, cnt = blk
            for tg in range(4):
                i, bk = pr.acquire()
                if tg == 0:
                    PE.wait_ge(ws.cast.sem, cnt)
                for t4 in range(4):
                    t = tg * 4 + t4
                    for k in range(8):
                        ins = PE.matmul(PS[:, bk, t4 * 128:(t4 + 1) * 128], hT[:, k, 1 + t * 128:1 + (t + 1) * 128],
                                        Wsl[:, slot, k, :], start=(k == 0), stop=(k == 7), skip_group_check=True)
                e_pe.s(ins)
                e_pe.w(ACT)
                e_vc.s(ACT.copy(dst_fn(tg), view(PS[:, bk, :])))
                pr.release(i, (e_vc, e_vc.n))
            wrel[bi] = e_pe.n

        cc_k, cc_v, cc_e, cc_u = ev("cck"), ev("ccv"), ev("cce"), ev("ccu")

        def act_gk():
            e_out.w(SP)
            e_go.s(SP.dma_start(out=gk_in[l].ap().rearrange("(h p) t -> p h t", p=128), in_=Kown), 16)

        def act_gv():
            e_vc.w(SP)
            e_out.w(SP)
            e_go.s(SP.dma_start(out=gv_in[l].ap().rearrange("(h p) f -> p h f", p=128),
                                in_=yT[:, 8 * T:8 * T + 4 * 16 * 129].rearrange("p (h f) -> p h f", h=4)), 16)
            e_go.s(SP.dma_start(out=ge_in[l][:, 0:128], in_=SKx[:, 128:256]), 16)
            e_go.s(SP.dma_start(out=ge_in[l][:, 128:256], in_=SKx[:, 16 * 128:17 * 128]), 16)
            e_go.s(SP.dma_start(out=ge_in[l][:, 256:386], in_=SVx[:, 1, :, :].rearrange("p g d -> p (g d)")), 16)
            e_go.s(SP.dma_start(out=ge_in[l][:, 386:516], in_=SVx[:, 16, :, :].rearrange("p g d -> p (g d)")), 16)
            e_go.w(POOL)
            for (cev, gi, go_) in ((cc_k, gk_in[l], gk_out[l]), (cc_v, gv_in[l], gv_out[l]), (cc_e, ge_in[l], ge_out[l])):
                POOL.collective_compute("AllGather", ALU.bypass, replica_groups=RG,
                                        ins=[gi.ap().opt()], outs=[go_.ap().opt()]).then_inc(cev.sem)
                cev.n = 1

        def act_gu():
            e_ue.w(SP)
            e_go.s(SP.dma_start(out=gu_in[l][:, :], in_=uedge[:]), 16)
            e_go.w(POOL)
            POOL.collective_compute("AllGather", ALU.bypass, replica_groups=RG,
                                    ins=[gu_in[l].ap().opt()], outs=[gu_out[l].ap().opt()]).then_inc(cc_u.sem)
            cc_u.n = 1

        axst = {}

        def proc_ax(blk, cc):
            axst[cc] = blk

        def proc_ac(blk, cc):
            b_ax, b_ac = axst[cc], blk
            for tc in range(4):
                i1, bk1 = pr.acquire()
                if tc == 0:
                    PE.wait_ge(ws.cast.sem, b_ac[2])
                for k in range(8):
                    ins = PE.matmul(PS[:, bk1, :], Wsl[:, b_ax[1], k, :], hT[:, k, 1 + tc * 512:1 + (tc + 1) * 512],
                                    start=(k == 0), stop=(k == 7))
                i2, bk2 = pr.acquire()
                for k in range(8):
                    ins = PE.matmul(PS[:, bk2, :], Wsl[:, b_ac[1], k, :], hT[:, k, 1 + tc * 512:1 + (tc + 1) * 512],
                                    start=(k == 0), stop=(k == 7))
                e_pe.s(ins)
                sl = e_ac.n % 2
                e_pe.w(ACT)
                e_u.w(ACT, e_u.n - 1)
                e_ac.s(ACT.copy(tmpc[:, sl, :], PS[:, bk2, :]))
                pr.release(i2, (e_ac, e_ac.n))
                e_ac.w(DVE)
                e_u.s(DVE.tensor_tensor(ubuf[:, 1 + tc * 512:1 + (tc + 1) * 512], PS[:, bk1, :], tmpc[:, sl, :], ALU.mult))
                pr.release(i1, (e_u, e_u.n))
            wrel[b_ax[0]] = e_pe.n
            wrel[b_ac[0]] = e_pe.n
            e_u.w(POOL)
            POOL.tensor_copy(uedge[:, 2 * cc:2 * cc + 1], ubuf[:, 1:2])
            e_ue.s(POOL.tensor_copy(uedge[:, 2 * cc + 1:2 * cc + 2], ubuf[:, T:T + 1]))

        def proc_ab(blk, cc):
            b_ab = blk
            for tc in range(4):
                i, bk = pr.acquire()
                if tc == 0:
                    PE.wait_ge(ws.cast.sem, b_ab[2])
                for k in range(8):
                    ins = PE.matmul(PS[:, bk, :], Wsl[:, b_ab[1], k, :], hT[:, k, 1 + tc * 512:1 + (tc + 1) * 512],
                                    start=(k == 0), stop=(k == 7))
                e_pe.s(ins)
                t0 = tc * 512
                e_u.w(DVE)
                e_y.w(DVE)
                e_c.s(DVE.tensor_scalar(cb[:, 0, :], ubuf[:, 1 + t0:513 + t0], convw[:, cc * 3 + 1:cc * 3 + 2], None, ALU.mult))
                e_c.w(DVE)
                e_c.s(DVE.scalar_tensor_tensor(cb[:, 1, :], ubuf[:, t0:512 + t0], convw[:, cc * 3:cc * 3 + 1], cb[:, 0, :],
                                               ALU.mult, ALU.add))
                e_c.w(DVE)
                e_c.s(DVE.scalar_tensor_tensor(cb[:, 2, :], ubuf[:, 2 + t0:514 + t0], convw[:, cc * 3 + 2:cc * 3 + 3], cb[:, 1, :],
                                               ALU.mult, ALU.add))
                e_c.w(DVE)
                e_pe.w(DVE)
                if tc == 0:
                    DVE.tensor_copy(abe[:, 2 * cc:2 * cc + 1], PS[:, bk, 0:1])
                    DVE.tensor_copy(c3e[:, 2 * cc:2 * cc + 1], cb[:, 2, 0:1])
                if tc == 3:
                    DVE.tensor_copy(abe[:, 2 * cc + 1:2 * cc + 2], PS[:, bk, 511:512])
                    DVE.tensor_copy(c3e[:, 2 * cc + 1:2 * cc + 2], cb[:, 2, 511:512])
                e_y.s(DVE.tensor_tensor(yaT[:, cc, t0:t0 + 512], PS[:, bk, :], cb[:, 2, :], ALU.mult))
                pr.release(i, (e_y, e_y.n))
            wrel[b_ab[0]] = e_pe.n

        sched = []
        for h in range(4):
            sched.append(([(O_DK + h * 128, 128, 0)], lambda blk, h=h: rope_block(blk, lambda tc: Kown[:, h, tc * 512:(tc + 1) * 512])))
        sched.append(([(O_SK, 128, 0)], lambda blk: rope_block(blk, lambda tc: SKx[:, 128 + tc * 512:128 + (tc + 1) * 512])))
        sched.append((None, act_gk))
        for h in range(4):
            sched.append(([(O_DV + h * 128, 128, 0)],
                          lambda blk, h=h: v_block(blk, lambda tg: Vown[:, h, tg * 4:(tg + 1) * 4, 0:128],
                                                   lambda p: p.rearrange("p (a b) -> p a b", a=4))))
        sched.append(([(O_SV, 128, 0)],
                      lambda blk: v_block(blk, lambda tg: SVx[:, 1 + tg * 4:1 + (tg + 1) * 4, :, 0:64],
                                          lambda p: p.rearrange("p (a g d) -> p a g d", a=4, g=2))))
        sched.append((None, act_gv))
        for cc in range(4):
            sched.append(([(O_AX + cc * 128, 128, 0)], lambda blk, cc=cc: proc_ax(blk, cc)))
            sched.append(([(O_AC + cc * 128, 128, 0)], lambda blk, cc=cc: proc_ac(blk, cc)))
            sched.append(([(O_AB + cc * 128, 128, 0)], lambda blk, cc=cc: proc_ab(blk, cc)))
        sched.append((None, act_gu))
        for i_ in range(4):
            sched.append(([(O_SQ + i_ * 64, 64, 0), (O_SQ + (4 + i_) * 64, 64, 64)],
                          lambda blk, i_=i_: rope_block(blk, lambda tc: SQT[:, i_, tc * 512:(tc + 1) * 512])))
        for h in range(4):
            sched.append(([(O_DQ + h * 128, 128, 0)], lambda blk, h=h: rope_block(blk, lambda tc: QT[:, h, tc * 512:(tc + 1) * 512])))
        wlist = [e_ for e_ in sched if e_[0] is not None]
        fetched = []

        def try_fetch():
            while len(fetched) < len(wlist):
                bi = len(fetched)
                if bi >= NWS and (bi - NWS) not in wrel:
                    break
                fetched.append(fetch_in(wlist[bi][0]))
            if len(fetched) == len(wlist):
                ws.flush()

        try_fetch()
        wi = 0
        for (segs, fn) in sched:
            if segs is None:
                fn()
            else:
                fn(fetched[wi])
                wi += 1
                try_fetch()
        barrier()
        r_ = stage_dump(f"K{l}", [PA, WIN, QS, MIX, LY], Kown, [128, 4, T], False)
        if r_ is not None:
            return r_
        r_ = stage_dump(f"Q{l}", [PA, WIN, QS, MIX, LY], QT[:], [128, 4, T], False)
        if r_ is not None:
            return r_
        r_ = stage_dump(f"YA{l}", [PA, WIN, QS, MIX, LY], yaT, [128, 4, T], False)
        if r_ is not None:
            return r_
        r_ = stage_dump(f"SK{l}", [PA, WIN, QS, MIX, LY], SKx[:], [128, 18 * 128], False)
        if r_ is not None:
            return r_
        PA.close()
        WB = ExitStack()
        guS = sbuf(WB, f"guS{l}", [128, 8, 8], F32)
        gut = sbuf(WB, f"gut{l}", [128, 2, 8, 8], F32)
        guh = sbuf(WB, f"guh{l}", [128, 2, 8], F32)
        yfx = sbuf(WB, f"yfx{l}", [128, 3, 8], F32)
        geS = sbuf(WB, f"geS{l}", [128, 8, 516], BF16)
        tmpS = sbuf(WB, f"tmpS{l}", [128, 8, 516], F32)
        red = sbuf(WB, f"red{l}", [128, 2, 516], F32)
        PTw = sbuf(WB, f"PTw{l}", [128, 2, 3, 512], BF16)
        Owsb = sbuf(WB, f"Owsb{l}", [128, 2, 4, 65], F32)
        lw = sbuf(WB, f"lw{l}", [128, 2, 2, 4], F32)
        ycw = sbuf(WB, f"ycw{l}", [128, 2, 256], F32)
        cc_u.w(SP, 1)
        cc_e.w(SP, 1)
        load_wait([(guS[:], gu_out[l].ap().rearrange("(r p) c -> p r c", p=128)),
                   (geS[:], ge_out[l].ap().rearrange("(r p) c -> p r c", p=128))], [DVE])
        for side in range(2):
            for r in range(8):
                ins = DVE.tensor_scalar(gut[:, side, r, :], guS[:, r, :], selv[:, side * 8 + r:side * 8 + r + 1], None, ALU.mult)
        e_fx.s(ins)
        e_fx.w(DVE)
        for side in range(2):
            ins = DVE.tensor_reduce(guh[:, side, :], gut[:, side, :, :].rearrange("p r c -> p c r"), AX.X, ALU.add)
        e_fx.s(ins)
        e_fx.w(DVE)
        cw3 = convw[:].rearrange("p (c j) -> p c j", j=3)
        gh3 = guh[:].rearrange("p s (c e) -> p s c e", e=2)
        c33 = c3e[:].rearrange("p (c e) -> p c e", e=2)
        ab3 = abe[:].rearrange("p (c e) -> p c e", e=2)
        yf3 = yfx[:].rearrange("p a (c e) -> p a c e", e=2)
        DVE.tensor_tensor(yf3[:, 0, :, 0], cw3[:, :, 0], gh3[:, 0, :, 1], ALU.mult)
        e_fx.s(DVE.tensor_tensor(yf3[:, 0, :, 1], cw3[:, :, 2], gh3[:, 1, :, 0], ALU.mult))
        e_fx.w(DVE)
        e_fx.s(DVE.tensor_tensor(yfx[:, 1, :], yfx[:, 0, :], c3e[:], ALU.add))
        e_fx.w(DVE)
        e_fx.s(DVE.tensor_tensor(yfx[:, 2, :], yfx[:, 1, :], abe[:], ALU.mult))
        e_fx.w(DVE)
        yf2 = yfx[:, 2, :].rearrange("p (c e) -> p c e", e=2)
        DVE.tensor_copy(yaT[:, :, 0], yf2[:, :, 0])
        e_fx.s(DVE.tensor_copy(yaT[:, :, T - 1], yf2[:, :, 1]))

        for side in range(2):
            for r in range(8):
                ins = DVE.tensor_scalar(tmpS[:, r, :], geS[:, r, :], selv[:, side * 8 + r:side * 8 + r + 1], None, ALU.mult)
            e_fx.s(ins)
            e_fx.w(DVE)
            e_fx.s(DVE.tensor_reduce(red[:, side, :], tmpS[:].rearrange("p r c -> p c r"), AX.X, ALU.add))
            e_fx.w(DVE)
        DVE.tensor_copy(SKx[:, 0:128], red[:, 0, 128:256])
        DVE.tensor_copy(SKx[:, 17 * 128:18 * 128], red[:, 1, 0:128])
        DVE.tensor_copy(SVx[:, 0, :, :].rearrange("p g d -> p (g d)"), red[:, 0, 386:516])
        e_fx.s(DVE.tensor_copy(SVx[:, 17, :, :].rearrange("p g d -> p (g d)"), red[:, 1, 256:386]))
        for e in (PE, ACT, POOL):
            e_fx.w(e)

        r_ = stage_dump(f"FX{l}", [WB, WIN, QS, MIX, LY], yaT, [128, 4, T], False)
        if r_ is not None:
            return r_
        prevtr = None
        n = 0
        for qb in range(NT):
            for g in range(2):
                sl = n % 2
                gs = slice(g * 64, (g + 1) * 64)
                e_wexp.w(PE, e_wexp.n - 1)
                for kk in range(3):
                    kb = qb + kk
                    ins = PE.matmul(PS[:, sl * 3 + kk, :], SKx[gs, kb * 128:(kb + 1) * 128],
                                    SQT[gs, :, qb * 128:(qb + 1) * 128], start=True, stop=True)
                e_ws.s(ins)
                e_ws.w(ACT)
                e_wpv.w(ACT, e_wpv.n - 1)
                e_wexp.s(ACT.activation(PTw[:, sl, :, :], PS[:, sl * 3:sl * 3 + 3, :], AF.Exp, scale=0.125))
                e_wexp.w(DVE)
                DVE.tensor_tensor(PTw[:, sl, 0, :], PTw[:, sl, 0, :], maskb[:, 0, :], ALU.mult)
                e_wm.s(DVE.tensor_tensor(PTw[:, sl, 2, :], PTw[:, sl, 2, :], maskb[:, 1, :], ALU.mult))
                e_wm.w(PE)
                e_wo.w(PE)
                first = True
                for i_ in range(4):
                    for kk in range(3):
                        kb = qb + kk
                        ins = PE.matmul(PS[:, 6, i_ * 65:(i_ + 1) * 65], PTw[:, sl, kk, i_ * 128:(i_ + 1) * 128],
                                        SVx[:, kb, g, :], start=first, stop=(kk == 2), skip_group_check=True)
                        first = False
                e_wpv.s(ins)
                if prevtr is not None:
                    prevtr()
                e_wpv.w(DVE)
                e_wtr.w(DVE, e_wtr.n - 1)
                e_wo.s(DVE.tensor_copy(Owsb[:, sl, :, :], PS[:, 6, 0:260].rearrange("p (i d) -> p i d", i=4)))
                e_wo.w(DVE)
                e_w1.s(DVE.tensor_tensor(lw[:, sl, 0, :], Owsb[:, sl, :, 64], esink[:, g * 4:(g + 1) * 4], ALU.add))
                e_w1.w(DVE)
                e_w1.s(DVE.reciprocal(lw[:, sl, 1, :], lw[:, sl, 0, :]))
                e_w1.w(DVE)
                for i_ in range(4):
                    ins = DVE.tensor_scalar(ycw[:, sl, i_ * 64:(i_ + 1) * 64], Owsb[:, sl, i_, 0:64], lw[:, sl, 1, i_:i_ + 1],
                                            None, ALU.mult)
                e_wy.s(ins)

                def tr(sl=sl, g=g, qb=qb, cnt=e_wy.n):
                    e_wy.w(PE, cnt)
                    e_tcp.w(PE)
                    for ii in range(2):
                        ins_ = PE.transpose(PS[:, 7, ii * 128:(ii + 1) * 128], ycw[:, sl, ii * 128:(ii + 1) * 128], ident[:])
                    e_wtr.s(ins_)
                    e_wtr.w(ACT)
                    e_tcp.s(ACT.copy(ycT[:, 2 * g:2 * g + 2, qb * 128:(qb + 1) * 128],
                                     PS[:, 7, 0:256].rearrange("p (a b) -> p a b", a=2)))
                prevtr = tr
                n += 1
        prevtr()
        barrier()
        r_ = stage_dump(f"YC{l}", [WB, WIN, QS, MIX, LY], yT[:, 0:12 * T], [128, 12 * T], False)
        if r_ is not None:
            return r_
        WB.close()
        WIN.close()

        DA = ExitStack()
        KVk = sbuf(DA, f"KVk{l}", [128, 2, T], BF16)
        KVv = sbuf(DA, f"KVv{l}", [128, 2, 16 * 129], BF16)
        PT = sbuf(DA, f"PT{l}", [128, 2, 1024], BF16)
        Osb = sbuf(DA, f"Osb{l}", [128, 9, 129], F32)
        rl = sbuf(DA, f"rl{l}", [128, 2, 8], F32)
        t1 = sbuf(DA, f"t1{l}", [128, 4, 128], F32)
        ob = sbuf(DA, f"ob{l}", [128, 4, 128], F32)
        junk = sbuf(DA, f"junk{l}", [128, 128], F32)
        ssb = sbuf(DA, f"ssb{l}", [128, 3, 4], F32)
        ybb = sbuf(DA, f"ybb{l}", [128, 4, 128], F32)
        cc_k.w(SP, 1)
        cc_v.w(SP, 1)
        if dbg is not None and dbg[0] == f"KV{l}":
            load_wait([(KVk[:, 0, :], gk_out[l][3 * 512 + 128:3 * 512 + 256, :]), (KVv[:, 0, :], gv_out[l][3 * 512 + 128:3 * 512 + 256, :])], [DVE])
            barrier()
            return dbg_dump([DA, QS, MIX, LY], KVk[:, 0, :], [128, T], False)
        blk_i = 0
        blk_end = {}
        pend_pv = None
        pend_tr = None
        stepc = 0

        def oacc(a):
            return PS[:, 4 + a // 3, (a % 3) * 129:(a % 3) * 129 + 129]

        hs_list = [(h_, s__) for h_ in range(4) for s__ in range(4)]
        if dbg is not None and len(dbg) > 3:
            hs_list = hs_list[:dbg[3]]
        for (h, s_) in hs_list:
            if True:
                for r in range(8 if (dbg is None or len(dbg) < 5) else dbg[4]):
                    slot = blk_i % 2
                    if blk_i >= 2:
                        e_pv.w(SP, blk_end[blk_i - 2])
                    row0 = r * 512 + h * 128
                    kvs[slot].s(SP.dma_start(out=KVk[:, slot, :], in_=gk_out[l][row0:row0 + 128, :]), 16)
                    kvs[slot].s(SP.dma_start(out=KVv[:, slot, :], in_=gv_out[l][row0:row0 + 128, :]), 16)
                    kv_need = kvs[slot].n
                    for kc in range(16):
                        sl = stepc % 2
                        first_hs = (r == 0 and kc == 0)
                        last_hs = (r == (7 if (dbg is None or len(dbg) < 5) else dbg[4] - 1) and kc == 15)
                        if kc == 0:
                            kvs[slot].w(PE, kv_need)
                        e_exp.w(PE, e_exp.n - 1)
                        PE.matmul(PS[:, 2 * sl, :], KVk[0:64, slot, kc * 128:(kc + 1) * 128], QT[0:64, h, s_ * 512:(s_ + 1) * 512],
                                  start=True, stop=True)
                        e_qk.s(PE.matmul(PS[:, 2 * sl + 1, :], KVk[64:128, slot, kc * 128:(kc + 1) * 128],
                                         QT[64:128, h, s_ * 512:(s_ + 1) * 512], start=True, stop=True))
                        e_qk.w(ACT)
                        e_pv.w(ACT)
                        e_exp.s(ACT.activation(PT[:, sl, :].rearrange("p (a b) -> p a b", a=2), PS[:, 2 * sl:2 * sl + 2, :], AF.Exp, scale=0.125))
                        if pend_pv is not None:
                            pend_pv()

                        def pv(sl=sl, slot=slot, kc=kc, first_hs=first_hs, last_hs=last_hs, cnt=e_exp.n, bi=blk_i):
                            e_exp.w(PE, cnt)
                            e_qk.w(PE)
                            if first_hs:
                                e_oev.w(PE)
                            for j in range(2):
                                for qt in range(4):
                                    a = j * 4 + qt
                                    ins_ = PE.matmul(oacc(a), PT[:, sl, j * 512 + qt * 128:j * 512 + (qt + 1) * 128],
                                                     KVv[:, slot, kc * 129:(kc + 1) * 129],
                                                     start=(first_hs and a % 3 == 0), stop=last_hs, skip_group_check=True)
                            e_pv.s(ins_)
                            if kc == 15:
                                blk_end[bi] = e_pv.n
                        pend_pv = pv
                        stepc += 1
                        if pend_tr is not None and r == 1 and kc == 0:
                            pend_tr()
                            pend_tr = None
                    blk_i += 1
                pend_pv()
                pend_pv = None
                e_pv.w(DVE)
                for b_ in range(3):
                    ins = DVE.tensor_copy(Osb[:, 3 * b_:3 * b_ + 3, :], PS[:, 4 + b_, 0:387].rearrange("p (a e) -> p a e", a=3))
                e_oev.s(ins)
                e_oev.w(DVE)
                if dbg is not None and dbg[0] == f"OS{l}":
                    barrier()
                    return dbg_dump([DA, QS, MIX, LY], Osb[:], [128, 9, 129], False)
                e_n.s(DVE.reciprocal(rl[:, 0, :], Osb[:, 0:8, 128]))
                e_n.w(DVE)
                e_n.s(DVE.tensor_scalar(rl[:, 1, 0:4], rl[:, 0, 4:8], neglam[:, 0:1], None, ALU.mult))
                e_n.w(DVE)
                e_ytr.w(DVE)
                for qt in range(4):
                    ins = DVE.tensor_scalar(t1[:, qt, :], Osb[:, 4 + qt, 0:128], rl[:, 1, qt:qt + 1], None, ALU.mult)
                e_n.s(ins)
                e_n.w(DVE)
                for qt in range(4):
                    ins = DVE.scalar_tensor_tensor(ob[:, qt, :], Osb[:, qt, 0:128], rl[:, 0, qt:qt + 1], t1[:, qt, :],
                                                   ALU.mult, ALU.add)
                e_n.s(ins)
                e_n.w(DVE)
                for qt in range(4):
                    ins = DVE.scalar_tensor_tensor(junk[:], ob[:, qt, :], 1.0, ob[:, qt, :], ALU.mult, ALU.mult,
                                                   accum_out=ssb[:, 0, qt:qt + 1])
                e_n.s(ins)
                e_n.w(POOL)
                e_np.s(POOL.tensor_scalar(ssb[:, 1, :], ssb[:, 0, :], 1.0 / 128.0, LN_EPS, ALU.mult, ALU.add))
                e_np.w(POOL)
                e_np.s(POOL.tensor_tensor(ssb[:, 2, :], ssb[:, 1, :], mhalf[:, 0:4], ALU.pow))
                e_np.w(DVE)
                for qt in range(4):
                    ins = DVE.scalar_tensor_tensor(ybb[:, qt, :], ob[:, qt, :], ssb[:, 2, qt:qt + 1], gsub[:], ALU.mult, ALU.mult)
                e_yb.s(ins)

                def trb(h=h, s_=s_, cnt=e_yb.n):
                    e_yb.w(PE, cnt)
                    e_ycp.w(PE)
                    for qt in range(4):
                        ins_ = PE.transpose(PS[:, 7, qt * 128:(qt + 1) * 128], ybb[:, qt, :], ident[:])
                    e_ytr.s(ins_)
                    e_ytr.w(DVE)
                    e_ycp.s(DVE.tensor_copy(ybT[:, h, s_ * 512:(s_ + 1) * 512], PS[:, 7, :]))
                pend_tr = trb
        pend_tr()
        barrier()
        r_ = stage_dump(f"Y{l}", [DA, QS, MIX, LY], yT[:, 0:12 * T], [128, 12 * T], False)
        if r_ is not None:
            return r_
        DA.close()
        QS.close()
        B3 = ExitStack()
        mergedT = sbuf(B3, f"mergedT{l}", [128, 8, T], BF16)
        B3A = ExitStack()
        ws.setup(B3A, 3, 1024)
        NG = 12
        Wg = sbuf(B3A, f"Wg{l}", [128, NG, 8, 128], BF16)
        sg = sbuf(B3A, f"sg{l}", [128, 2, 3, 512], F32)
        prd = sbuf(B3A, f"prd{l}", [128, 2, 3, 512], F32)
        s01 = sbuf(B3A, f"s01{l}", [128, 2, 512], F32)
        wgf, wg_gate = WF[("w_branch_gate", l)][2], WF[("w_branch_gate", l)][3]
        wbf, wb_gate = WF[("w_branch", l)][2], WF[("w_branch", l)][3]
        pr = PRing([0, 1, 2, 3, 4, 5])
        gblk = [0]
        grel = {}

        def fetch_g(kind, dc, n_):
            bi = gblk[0]
            gblk[0] += 1
            slot = bi % NG
            dfree = (e_pe, grel[bi - NG]) if bi >= NG else None
            if kind == 0:
                src = wgf[:, n_ * 1024 + dc * 128:n_ * 1024 + (dc + 1) * 128].rearrange("(k p) c -> p k c", p=128)
                cnt = ws.fetch([(src, v3(8, 128))], Wg[:, slot, :, :], dst_free=dfree, gate=wg_gate)
            else:
                src = wbf[n_ * 512:(n_ + 1) * 512, dc * 128:(dc + 1) * 128].rearrange("(k p) c -> p k c", p=128)
                cnt = ws.fetch([(src, v3(4, 128))], Wg[:, slot, 0:4, :], dst_free=dfree, gate=wb_gate)
            return bi, slot, cnt

        glist = [(kind, dc, n_) for dc in range(8) for n_ in range(3) for kind in (0, 1)]
        gfetched = []

        def try_fetch_g():
            while len(gfetched) < len(glist):
                bi = len(gfetched)
                if bi >= NG and (bi - NG) not in grel:
                    break
                gfetched.append(fetch_g(*glist[bi]))
            if len(gfetched) == len(glist):
                ws.flush()

        try_fetch_g()
        yTn = [yaT, ybT, ycT]
        step = 0
        pr_cnt = {}
        mg_cnt = {}
        for dc in range(8):
            blks = gfetched[dc * 6:(dc + 1) * 6]
            for tc in range(4):
                sl2 = step % 2
                for n_ in range(3):
                    bg_, bb_ = blks[2 * n_], blks[2 * n_ + 1]
                    i1, bk1 = pr.acquire()
                    if tc == 0:
                        PE.wait_ge(ws.cast.sem, bb_[2])
                    for k in range(8):
                        ins = PE.matmul(PS[:, bk1, :], Wg[:, bg_[1], k, :], hT[:, k, 1 + tc * 512:1 + (tc + 1) * 512],
                                        start=(k == 0), stop=(k == 7))
                    e_pe.s(ins)
                    c_gate = e_pe.n
                    i2, bk2 = pr.acquire()
                    for k in range(4):
                        ins = PE.matmul(PS[:, bk2, :], Wg[:, bb_[1], k, :], yTn[n_][:, k, tc * 512:(tc + 1) * 512],
                                        start=(k == 0), stop=(k == 3))
                    e_pe.s(ins)
                    c_br = e_pe.n
                    e_pe.w(ACT, c_gate)
                    if (step - 2, n_) in pr_cnt:
                        e_pr.w(ACT, pr_cnt[(step - 2, n_)])
                    e_sg.s(ACT.activation(sg[:, sl2, n_, :], PS[:, bk1, :], AF.Sigmoid,
                                          bias=bgt[:, n_ * 8 + dc:n_ * 8 + dc + 1], scale=1.0))
                    pr.release(i1, (e_sg, e_sg.n))
                    e_sg.w(DVE)
                    e_pe.w(DVE, c_br)
                    if (step - 2) in mg_cnt:
                        e_mgd.w(DVE, mg_cnt[step - 2])
                    e_pr.s(DVE.tensor_tensor(prd[:, sl2, n_, :], sg[:, sl2, n_, :], PS[:, bk2, :], ALU.mult))
                    pr_cnt[(step, n_)] = e_pr.n
                    pr.release(i2, (e_pr, e_pr.n))
                e_pr.w(POOL)
                e_s01.s(POOL.tensor_tensor(s01[:, sl2, :], prd[:, sl2, 0, :], prd[:, sl2, 1, :], ALU.add))
                e_s01.w(POOL)
                e_mgd.s(POOL.tensor_tensor(mergedT[:, dc, tc * 512:(tc + 1) * 512], s01[:, sl2, :], prd[:, sl2, 2, :], ALU.add))
                mg_cnt[step] = e_mgd.n
                step += 1
            for b_ in blks:
                grel[b_[0]] = e_pe.n
            try_fetch_g()
        barrier()
        B3A.close()

        B3B = ExitStack()
        ws.setup(B3B, 2, 1024)
        Wo = sbuf(B3B, f"Wo{l}", [128, 8, D], BF16)
        ln_setup(B3B, lnm_g_d[l:l + 1, :], lnm_b_d[l:l + 1, :])
        hr = sbuf(B3B, f"hr{l}", [128, 2, D], F32)
        pre = sbuf(B3B, f"pre{l}", [128, 2, D], F32)
        wof, wo_gate = WF[("w_o", l)][2], WF[("w_o", l)][3]
        for cb_ in range(8):
            src = wof[:, cb_ * 128:(cb_ + 1) * 128].rearrange("(k p) c -> p k c", p=128)
            wo_cnt = ws.fetch([(src, v3(8, 128))], Wo[:, :, cb_ * 128:(cb_ + 1) * 128], gate=wo_gate)
        ws.flush()
        PE.wait_ge(ws.cast.sem, wo_cnt)
        pr2 = PRing([0, 2, 4])
        cc_2 = ev("cc2")
        order = [0, NT - 1] + list(range(1, NT - 1))
        for ti, t in enumerate(order):
            sl = ti % 2
            e_pre.w(SP, e_pre.n - 1)
            hrs[sl].s(SP.dma_start(out=hr[:, sl, :], in_=hres_d[t * 128:(t + 1) * 128, :]), 16)
            hr_need = hrs[sl].n
            i, bk = pr2.acquire()
            for half in range(2):
                for k in range(8):
                    ins = PE.matmul(PS[:, bk + half, :], mergedT[:, k, t * 128:(t + 1) * 128], Wo[:, k, half * 512:(half + 1) * 512],
                                    start=(k == 0), stop=(k == 7))
            e_pe.s(ins)
            e_pe.w(DVE)
            hrs[sl].w(DVE, hr_need)
            L.e_nrm.w(DVE, L.e_nrm.n - 1)
            e_pre.s(DVE.scalar_tensor_tensor(pre[:, sl, :].rearrange("p (a b) -> p a b", a=2),
                                             hr[:, sl, :].rearrange("p (a b) -> p a b", a=2), ALPHA,
                                             PS[:, bk:bk + 2, :], ALU.mult, ALU.add))
            pr2.release(i, (e_pre, e_pre.n))
            lsl = ln_tile(pre[:, sl, :], (e_pre, e_pre.n), t, hres_d)
            if t == 0:
                L.e_ho.w(SP)
                L.e_st.s(SP.dma_start(out=e2_in[l][0:1, :], in_=L.ho[0:1, lsl, :]), 16)
            if t == NT - 1:
                L.e_ho.w(SP)
                L.e_st.s(SP.dma_start(out=e2_in[l][1:2, :], in_=L.ho[127:128, lsl, :]), 16)
                L.e_st.w(POOL)
                POOL.collective_compute("AllGather", ALU.bypass, replica_groups=RG,
                                        ins=[e2_in[l].ap().opt()], outs=[e2_out[l].ap().opt()]).then_inc(cc_2.sem)
                cc_2.n = 1
        L.e_st.w(SP)
        barrier()
        r_ = stage_dump(f"HM{l}", [B3B, B3, MIX, LY], hT[:], [128, 8, T + 2], False)
        if r_ is not None:
            return r_
        B3B.close()
        B3.close()
        MIX.close()

        last = (l == DEPTH - 1)
        dst_d = out_d if last else hres_d
        with ExitStack() as st:
            e2S = sbuf(st, f"e2S{l}", [16, D], F32)
            e2b = sbuf(st, f"e2b{l}", [16, D], BF16)
            cc_2.w(SP, 1)
            load_wait([(e2S[:], e2_out[l][:, :])], [DVE])
            e_fh.s(DVE.tensor_copy(e2b[:], e2S[:]))
            e_fh.w(PE)
            for c in range(8):
                ins = PE.matmul(PS[:, 7, c * 2:(c + 1) * 2], e2b[0:16, c * 128:(c + 1) * 128], sel16b[0:16, 0:2], start=True, stop=True)
            e_fh.s(ins)
            e_fh.w(DVE)
            hv = PS[:, 7, 0:16].rearrange("p (c s) -> p c s", s=2)
            DVE.tensor_copy(hT[:, :, 0], hv[:, :, 0])
            e_fh.s(DVE.tensor_copy(hT[:, :, T + 1], hv[:, :, 1]))
            barrier()
        FF = ExitStack()
        Wd = sbuf(FF, f"Wd{l}", [128, NFF, D], BF16)
        gT = sbuf(FF, f"gT{l}", [128, NFF, 512], BF16)
        GU = sbuf(FF, f"GU{l}", [128, 2, 2, 514], F32)
        Hc = sbuf(FF, f"Hc{l}", [128, 2, 2, 512], F32)
        cbf = sbuf(FF, f"cbf{l}", [128, 2, 2, 512], F32)
        Wu = sbuf(FF, f"Wu{l}", [128, 4, 8, 128], BF16)
        hodef = sbuf(FF, f"hodef{l}", [128, D], F32)
        ws.setup(FF, 3, 1024, defer=2)
        ln_setup(FF, lnf_g_d[l:l + 1, :], lnf_b_d[l:l + 1, :])
        hr = sbuf(FF, f"hrf{l}", [128, 2, D], F32)
        pre = sbuf(FF, f"pref{l}", [128, 2, D], F32)
        wuf, wu_gate = WF[("w_ffn_up", l)][2], WF[("w_ffn_up", l)][3]
        wdf, wd_gate = WF[("w_ffn_down", l)][2], WF[("w_ffn_down", l)][3]
        for i_ in range(NFF):
            wd_cnt = ws.fetch([(wdf[i_ * 128:(i_ + 1) * 128, :], lambda s_: s_[:, 0:D])], Wd[:, i_, :], gate=wd_gate)
        ws.flush()
        ublk = [0]
        urel = {}

        def fetch_u(i_):
            bi = ublk[0]
            ublk[0] += 1
            sl0 = (bi % 2) * 2
            dfree = (e_pe, urel[bi - 2]) if bi >= 2 else None
            srcg = wuf[:, i_ * 128:(i_ + 1) * 128].rearrange("(k p) c -> p k c", p=128)
            srcu = wuf[:, D_FF + i_ * 128:D_FF + (i_ + 1) * 128].rearrange("(k p) c -> p k c", p=128)
            ws.fetch([(srcg, v3(8, 128))], Wu[:, sl0, :, :], dst_free=dfree, gate=wu_gate)
            cnt = ws.fetch([(srcu, v3(8, 128))], Wu[:, sl0 + 1, :, :], dst_free=dfree)
            return bi, sl0, cnt

        fcw3 = fcw[:].rearrange("p (m j) -> p m j", j=3)
        pru = PRing([0, 2, 4])
        prd2 = PRing([0, 2, 4])
        pend_def = None
        ustep = 0
        for q in range(4):
            c0 = q * 512
            ulist = [fetch_u(0), fetch_u(1)]
            ws.flush()
            for i_ in range(NFF):
                bi, sl0, cnt = ulist[i_]
                gsl = ustep % 2
                i, bk = pru.acquire()
                PE.wait_ge(ws.cast.sem, cnt)
                if q == 0 and i_ == 0:
                    e_fh.w(PE)
                for wh in range(2):
                    for k in range(8):
                        PE.matmul(PS[:, bk + wh, :], Wu[:, sl0 + wh, k, :], hT[:, k, c0:c0 + 512], start=(k == 0), stop=(k == 7))
                e_gu.w(PE)
                for wh in range(2):
                    for k in range(8):
                        ins = PE.matmul(PS[:, 6, wh * 2:wh * 2 + 2], Wu[:, sl0 + wh, k, :], hT[:, k, c0 + 512:c0 + 514],
                                        start=(k == 0 and wh == 0), stop=(k == 7), skip_group_check=True)
                e_pe.s(ins)
                urel[bi] = e_pe.n
                if i_ + 2 < NFF:
                    ulist.append(fetch_u(i_ + 2))
                if i_ + 3 >= NFF:
                    ws.flush()
                if pend_def is not None and i_ == 2:
                    pend_def()
                    pend_def = None
                e_pe.w(ACT)
                e_c.w(ACT, e_c.n - 4)
                ACT.copy(GU[:, gsl, :, 0:512], PS[:, bk:bk + 2, :])
                e_gu.s(ACT.copy(GU[:, gsl, :, 512:514], PS[:, 6, 0:4].rearrange("p (w c) -> p w c", w=2)))
                pru.release(i, (e_gu, e_gu.n))
                e_gu.w(DVE)
                e_g.w(DVE, e_g.n - 1)
                ms = [i_, NFF + i_]
                for wh in range(2):
                    ins = DVE.tensor_scalar(cbf[:, 0, wh, :], GU[:, gsl, wh, 1:513], fcw3[:, ms[wh], 1:2], None, ALU.mult)
                e_c.s(ins)
                e_c.w(DVE)
                for wh in range(2):
                    ins = DVE.scalar_tensor_tensor(cbf[:, 1, wh, :], GU[:, gsl, wh, 0:512], fcw3[:, ms[wh], 0:1], cbf[:, 0, wh, :],
                                                   ALU.mult, ALU.add)
                e_c.s(ins)
                e_c.w(DVE)
                for wh in range(2):
                    ins = DVE.scalar_tensor_tensor(Hc[:, gsl, wh, :], GU[:, gsl, wh, 2:514], fcw3[:, ms[wh], 2:3], cbf[:, 1, wh, :],
                                                   ALU.mult, ALU.add)
                    e_c.s(ins)
                e_c.w(ACT, e_c.n - 1)
                e_gl.s(ACT.activation(Hc[:, gsl, 0, :], Hc[:, gsl, 0, :], AF.Gelu))
                e_gl.w(POOL)
                e_c.w(POOL)
                if i_ == 0:
                    e_pe.w(POOL)
                e_g.s(POOL.tensor_tensor(gT[:, i_, :], Hc[:, gsl, 0, :], Hc[:, gsl, 1, :], ALU.mult))
                ustep += 1
            e_g.w(PE)
            PE.wait_ge(ws.cast.sem, wd_cnt)
            for t4 in range(4):
                t = q * 4 + t4
                sl = e_pre.n % 2
                e_pre.w(SP, e_pre.n - 1)
                hrs[sl].s(SP.dma_start(out=hr[:, sl, :], in_=hres_d[t * 128:(t + 1) * 128, :]), 16)
                hr_need = hrs[sl].n
                i, bk = prd2.acquire()
                if t4 == 0:
                    for (e__, c__) in pru.rel.get(pru.i - 1, []) + pru.rel.get(pru.i - 2, []) + pru.rel.get(pru.i - 3, []):
                        e__.w(PE, c__)
                for half in range(2):
                    for i_ in range(NFF):
                        ins = PE.matmul(PS[:, bk + half, :], gT[:, i_, t4 * 128:(t4 + 1) * 128], Wd[:, i_, half * 512:(half + 1) * 512],
                                        start=(i_ == 0), stop=(i_ == NFF - 1))
                e_pe.s(ins)
                e_pe.w(DVE)
                hrs[sl].w(DVE, hr_need)
                L.e_nrm.w(DVE, L.e_nrm.n - 1)
                e_pre.s(DVE.scalar_tensor_tensor(pre[:, sl, :].rearrange("p (a b) -> p a b", a=2),
                                                 hr[:, sl, :].rearrange("p (a b) -> p a b", a=2), ALPHA,
                                                 PS[:, bk:bk + 2, :], ALU.mult, ALU.add))
                prd2.release(i, (e_pre, e_pre.n))
                deferred = (not last) and t4 == 3 and q < 3
                lsl = ln_tile(pre[:, sl, :], (e_pre, e_pre.n), t, dst_d, write_hT=(not last) and not deferred, tr_banks=(7,))
                if deferred:
                    L.e_ho.w(POOL)
                    e_hd.s(POOL.tensor_copy(hodef[:], L.ho[:, lsl, :]))

                    def dfn(t=t, cnt=e_hd.n):
                        ln_transposes(hodef[:], t, (7,), src_ready=(e_hd, cnt))
                    pend_def = dfn
            for j_ in range(1, 4):
                for (e__, c__) in prd2.rel.get(prd2.i - j_, []):
                    e__.w(PE, c__)
        L.e_st.w(SP)
        barrier()
        FF.close()
        LY.close()

    ES.close()
    return nc


def _consts(c):
    cvec = np.zeros((128, 4), np.float32)
    p = np.arange(128)
    inv = 1.0 / (THETA ** (np.arange(0, 64, 2, dtype=np.float32) / np.float32(64)))
    cvec[:, 0] = inv.astype(np.float32)[p % 32]
    cvec[:, 1] = np.where((p % 64) < 32, 1.0, -1.0)
    cvec[:, 2] = 1.0 if c > 0 else 0.0
    cvec[:, 3] = 1.0 if c < NCORES - 1 else 0.0
    selv = np.zeros((128, 16), np.float32)
    if c > 0:
        selv[:, c - 1] = 1.0
    if c < NCORES - 1:
        selv[:, 8 + c + 1] = 1.0
    sel16 = np.zeros((16, 2), np.float32)
    if c > 0:
        sel16[(c - 1) * 2 + 1, 0] = 1.0
    if c < NCORES - 1:
        sel16[(c + 1) * 2 + 0, 1] = 1.0
    ident = np.eye(128, dtype=np.float32)
    kj = np.arange(128)[:, None]
    qi = np.arange(128)[None, :]
    mp = (kj >= qi).astype(np.float32)
    mn = (kj <= qi).astype(np.float32)
    masks = np.stack([np.tile(mp, (1, 4)), np.tile(mn, (1, 4))], axis=1).astype(np.float32)
    return cvec, selv, sel16, ident, masks


def _in_maps(inputs):
    f = lambda a: np.ascontiguousarray(np.asarray(a, dtype=np.float32))
    x = f(inputs["x"]).reshape(SEQ, D)
    pos = np.ascontiguousarray(np.asarray(inputs["positions"], dtype=np.int32)).reshape(SEQ)
    shared = {
        "ln_in_g": f(inputs["ln_in_g"]).reshape(1, D), "ln_in_b": f(inputs["ln_in_b"]).reshape(1, D),
        "conv_w": f(inputs["conv_w"]),
        "diff_lambda": f(inputs["diff_lambda"]).reshape(DEPTH, 256),
        "diff_subln_g": f(inputs["diff_subln_g"]), "swa_sink": f(inputs["swa_sink"]),
        "b_branch_gate": f(inputs["b_branch_gate"]),
        "ln_mix_g": f(inputs["ln_mix_g"]), "ln_mix_b": f(inputs["ln_mix_b"]),
        "ffn_conv_w": f(inputs["ffn_conv_w"]),
        "ln_ffn_g": f(inputs["ln_ffn_g"]), "ln_ffn_b": f(inputs["ln_ffn_b"]),
    }
    big = {}
    for (nm, R, C) in BIGW:
        w = f(inputs[nm]).reshape(DEPTH, R, C)
        for l in range(DEPTH):
            big[(nm, l)] = w[l]
    maps = []
    for c in range(NCORES):
        cvec, selv, sel16, ident, masks = _consts(c)
        m = dict(shared)
        for (nm, R, C) in BIGW:
            for l in range(DEPTH):
                rs = R // NCORES
                m[f"{nm}{l}"] = np.ascontiguousarray(big[(nm, l)][c * rs:(c + 1) * rs])
        m.update({"x": x[c * T:(c + 1) * T], "pos": pos[c * T:(c + 1) * T].reshape(1, T),
                  "cvec": cvec, "selv": selv, "sel16": sel16, "ident": ident, "masks": masks})
        maps.append(m)
    return maps


def kernel(**inputs):
    nc = build_nc()
    res = run_bass_kernel_spmd(nc, _in_maps(inputs), core_ids=list(range(NCORES)))
    out = np.concatenate([np.asarray(r["out"], dtype=np.float32) for r in res.results], axis=0)
    return out.reshape(1, SEQ, D)
```

```python
import math
from contextlib import ExitStack

import numpy as np
import concourse.bass as bass
import concourse.mybir as mybir
from concourse.bass_utils import run_bass_kernel_spmd

F32 = mybir.dt.float32
BF16 = mybir.dt.bfloat16
I32 = mybir.dt.int32
AF = mybir.ActivationFunctionType
ALU = mybir.AluOpType
AX = mybir.AxisListType

NCORES = 8
SEQ = 16384
T = SEQ // NCORES
NT = T // 128
D = 1024
DEPTH = 2
D_IN = 3840
D_FF = 2816
NFF = D_FF // 128
LN_EPS = 1e-5
ALPHA = (2 * DEPTH) ** 0.25
THETA = 10000.0
MAGIC = 12582912.0
TWO_PI = 2.0 * math.pi
C1 = 6.28125
C2 = float(np.float32(TWO_PI - C1))

BIGW = [("w_in", 1024, 3840), ("w_branch_gate", 1024, 3072), ("w_branch", 1536, 1024), ("w_o", 1024, 1024),
        ("w_ffn_up", 1024, 5632), ("w_ffn_down", 2816, 1024)]
O_AX, O_AB, O_AC, O_DQ, O_DK, O_DV, O_SQ, O_SK, O_SV = 0, 512, 1024, 1536, 2048, 2560, 3072, 3584, 3712


class Ev:
    def __init__(self, nc, es, name):
        self.sem = es.enter_context(nc.semaphore(name))
        self.n = 0

    def s(self, ins, k=1):
        ins.then_inc(self.sem, k)
        self.n += k
        return self.n

    def w(self, eng, v=None):
        v = self.n if v is None else v
        if v > 0:
            eng.wait_ge(self.sem, v)


def build_nc(dbg=None):
    nc = bass.Bass("TRN2", target_bir_lowering=False)
    ES = ExitStack()
    PE, ACT, DVE, POOL, SP = nc.tensor, nc.scalar, nc.vector, nc.gpsimd, nc.sync
    ENGS = [PE, ACT, DVE, POOL, SP]
    evc = [0]

    def ev(name):
        evc[0] += 1
        return Ev(nc, ES, f"{name}_{evc[0]}")

    def din(name, shape, dt=F32):
        return nc.dram_tensor(name, list(shape), dt, kind="ExternalInput")

    x_d = din("x", [T, D])
    pos_d = din("pos", [1, T], I32)
    cvec_d = din("cvec", [128, 4])
    selv_d = din("selv", [128, 16])
    sel16_d = din("sel16", [16, 2])
    ident_d = din("ident", [128, 128])
    mask_d = din("masks", [128, 2, 512])
    lnin_g_d = din("ln_in_g", [1, D])
    lnin_b_d = din("ln_in_b", [1, D])
    conv_w_d = din("conv_w", [DEPTH, 3, 512])
    dlam_d = din("diff_lambda", [DEPTH, 256])
    subg_d = din("diff_subln_g", [DEPTH, 128])
    sink_d = din("swa_sink", [DEPTH, 8])
    bg_d = din("b_branch_gate", [DEPTH, 3 * D])
    lnm_g_d = din("ln_mix_g", [DEPTH, D])
    lnm_b_d = din("ln_mix_b", [DEPTH, D])
    fcw_d = din("ffn_conv_w", [DEPTH, 3, 2 * D_FF])
    lnf_g_d = din("ln_ffn_g", [DEPTH, D])
    lnf_b_d = din("ln_ffn_b", [DEPTH, D])
    out_d = nc.dram_tensor("out", [T, D], F32, kind="ExternalOutput")
    dbg_d = None
    if dbg is not None:
        dbg_d = nc.dram_tensor("dbg", list(dbg[1]), F32 if len(dbg) < 3 else dbg[2], kind="ExternalOutput")

    hres_d = nc.dram_tensor("hres", [T, D], F32)
    WF = {}
    for l in range(DEPTH):
        for (nm, R, C) in BIGW:
            sh = din(f"{nm}{l}", [R // NCORES, C])
            bb = nc.dram_tensor(f"{nm}{l}_b", [R // NCORES, C], F32)
            ff = nc.dram_tensor(f"{nm}{l}_f", [R, C], F32, addr_space="Shared")
            WF[(nm, l)] = [sh, bb, ff, None]
    gk_in = [nc.dram_tensor(f"gk_in{l}", [512, T], BF16) for l in range(DEPTH)]
    gk_out = [nc.dram_tensor(f"gk_out{l}", [NCORES * 512, T], BF16, addr_space="Shared") for l in range(DEPTH)]
    gv_in = [nc.dram_tensor(f"gv_in{l}", [512, 16 * 129], BF16) for l in range(DEPTH)]
    gv_out = [nc.dram_tensor(f"gv_out{l}", [NCORES * 512, 16 * 129], BF16, addr_space="Shared") for l in range(DEPTH)]
    ge_in = [nc.dram_tensor(f"ge_in{l}", [128, 516], BF16) for l in range(DEPTH)]
    ge_out = [nc.dram_tensor(f"ge_out{l}", [NCORES * 128, 516], BF16, addr_space="Shared") for l in range(DEPTH)]
    gu_in = [nc.dram_tensor(f"gu_in{l}", [128, 8], F32) for l in range(DEPTH)]
    gu_out = [nc.dram_tensor(f"gu_out{l}", [NCORES * 128, 8], F32, addr_space="Shared") for l in range(DEPTH)]
    e2_in = [nc.dram_tensor(f"e2_in{l}", [2, D], F32) for l in range(DEPTH)]
    e2_out = [nc.dram_tensor(f"e2_out{l}", [NCORES * 2, D], F32, addr_space="Shared") for l in range(DEPTH)]

    sbc = [0]

    def sbuf(st, name, shape, dt):
        sbc[0] += 1
        return st.enter_context(nc.sbuf_tensor(f"s{sbc[0]}_{name}", list(shape), dt))

    hT = sbuf(ES, "hT", [128, 8, T + 2], BF16)
    CC = sbuf(ES, "CC", [128, T], F32)
    TS = sbuf(ES, "TS", [128, T], F32)
    cvec = sbuf(ES, "cvec", [128, 4], F32)
    selv = sbuf(ES, "selv", [128, 16], F32)
    sel16b = sbuf(ES, "sel16b", [16, 2], BF16)
    ident = sbuf(ES, "ident", [128, 128], F32)
    maskb = sbuf(ES, "maskb", [128, 2, 512], BF16)
    mhalf = sbuf(ES, "mhalf", [128, 8], F32)
    halfpi = sbuf(ES, "halfpi", [128, 1], F32)
    epsc = sbuf(ES, "epsc", [128, 1], F32)

    PS = ES.enter_context(nc.psum_tensor("PS", [128, 8, 512], F32))

    bar = ev("bar")

    def barrier():
        for e in ENGS:
            e.drain()
            e.sem_inc(bar.sem, 1)
        bar.n += len(ENGS)
        for e in ENGS:
            bar.w(e)

    gdma = ev("gdma")

    def load_wait(pairs, eng_list):
        for dst, src in pairs:
            gdma.s(SP.dma_start(out=dst, in_=src), 16)
        for e in eng_list:
            gdma.w(e)

    c0 = ev("c0")
    with ExitStack() as st:
        sel16f = sbuf(st, "sel16f", [16, 2], F32)
        maskf = sbuf(st, "maskf", [128, 2, 512], F32)
        load_wait([(cvec[:], cvec_d[:, :]), (selv[:], selv_d[:, :]), (sel16f[:], sel16_d[:, :]),
                   (ident[:], ident_d[:, :]), (maskf[:], mask_d[:, :, :])], [DVE, POOL, ACT, PE])
        DVE.tensor_copy(maskb[:], maskf[:])
        DVE.tensor_copy(sel16b[:], sel16f[:])
        DVE.memset(mhalf[:], -0.5)
        DVE.memset(halfpi[:], math.pi / 2.0)
        c0.s(DVE.memset(epsc[:], LN_EPS))
        for e in ENGS:
            c0.w(e)
        barrier()

    RG = [list(range(NCORES))]
    e1 = ev("wsh")
    for l in range(DEPTH):
        for (nm, R, C) in BIGW:
            sh, bb, ff, _ = WF[(nm, l)]
            e1.s(SP.dma_start(out=bb[:, :], in_=sh[:, :]), 16)
    e1.w(POOL)
    for l in range(DEPTH):
        for (nm, R, C) in BIGW:
            sh, bb, ff, _ = WF[(nm, l)]
            e2 = ev("wcc")
            POOL.collective_compute("AllGather", ALU.bypass, replica_groups=RG,
                                    ins=[bb.ap().opt()], outs=[ff.ap().opt()]).then_inc(e2.sem)
            e2.n = 1
            WF[(nm, l)][3] = e2

    with ExitStack() as st:
        posi = sbuf(st, "posi", [128, T], I32)
        ta = sbuf(st, "ta", [128, T], F32)
        tb = sbuf(st, "tb", [128, T], F32)
        tk = sbuf(st, "tk", [128, T], F32)
        load_wait([(posi[:], pos_d[0:1, :].partition_broadcast(128))], [DVE])
        ch = ev("rt")
        ch.s(DVE.tensor_copy(ta[:], posi[:]))
        ch.w(DVE)
        ch.s(DVE.tensor_scalar(tb[:], ta[:], cvec[:, 0:1], None, ALU.mult))
        ch.w(DVE)
        ch.s(DVE.tensor_scalar(tk[:], tb[:], 1.0 / TWO_PI, MAGIC, ALU.mult, ALU.add))
        ch.w(DVE)
        ch.s(DVE.tensor_scalar(ta[:], tk[:], -MAGIC, None, ALU.add))
        ch.w(DVE)
        ch.s(DVE.scalar_tensor_tensor(tk[:], ta[:], -C1, tb[:], ALU.mult, ALU.add))
        ch.w(DVE)
        ch.s(DVE.scalar_tensor_tensor(tb[:], ta[:], -C2, tk[:], ALU.mult, ALU.add))
        ch.w(DVE)
        ch.s(DVE.tensor_scalar(tk[:], tb[:], -1.0, None, ALU.mult))
        ch.w(DVE)
        ch.s(DVE.tensor_tensor(ta[:], tk[:], tb[:], ALU.max))
        ch.w(ACT)
        ACT.activation(TS[:], tb[:], AF.Sin, scale=cvec[:, 1:2])
        ch.s(ACT.activation(CC[:], ta[:], AF.Sin, bias=halfpi[:, 0:1], scale=-1.0))
        barrier()

    class LNState:
        pass

    L = LNState()
    L.e_stat, L.e_aggr, L.e_p1, L.e_p2, L.e_p3 = ev("lns"), ev("lna"), ev("lnp1"), ev("lnp2"), ev("lnp3")
    L.e_nrm, L.e_mg, L.e_ho, L.e_tr, L.e_cp, L.e_st = ev("lnn"), ev("lnm"), ev("lnh"), ev("lnt"), ev("lnc"), ev("lnst")
    L.cnt = 0

    def ln_setup(st, g_src, b_src):
        L.g = sbuf(st, "ln_g", [128, D], F32)
        L.b = sbuf(st, "ln_b", [128, D], F32)
        L.stats = sbuf(st, "ln_stats", [128, 2, 2, 6], F32)
        L.mv = sbuf(st, "ln_mv", [128, 2, 2], F32)
        L.sm = sbuf(st, "ln_sm", [128, 2, 4], F32)
        L.nrm = sbuf(st, "ln_nrm", [128, 2, D], F32)
        L.ho = sbuf(st, "ln_ho", [128, 2, D], F32)
        load_wait([(L.g[:], g_src.partition_broadcast(128)), (L.b[:], b_src.partition_broadcast(128))], [DVE, POOL])

    L.p3_cnt, L.nrm_cnt, L.ho_cnt, L.st_cnt, L.tr_cnt, L.extra = {}, {}, {}, {}, {}, {}

    def ln_front(pre, pre_ready):
        i = L.cnt
        sl = i % 2
        L.cnt += 1
        pre_ready[0].w(DVE, pre_ready[1])
        if i - 2 in L.p3_cnt:
            L.e_p3.w(DVE, L.p3_cnt[i - 2])
        DVE.bn_stats(L.stats[:, sl, 0, :], pre[:, 0:512])
        L.e_stat.s(DVE.bn_stats(L.stats[:, sl, 1, :], pre[:, 512:1024]))
        L.e_stat.w(DVE)
        L.e_aggr.s(DVE.bn_aggr(L.mv[:, sl, :], L.stats[:, sl, :, :]))
        L.e_aggr.w(POOL)
        if i - 2 in L.nrm_cnt:
            L.e_nrm.w(POOL, L.nrm_cnt[i - 2])
        L.e_p1.s(POOL.tensor_scalar(L.sm[:, sl, 0:1], L.mv[:, sl, 1:2], LN_EPS, None, ALU.add))
        L.e_p1.w(POOL)
        L.e_p2.s(POOL.tensor_tensor(L.sm[:, sl, 1:2], L.sm[:, sl, 0:1], mhalf[:, 0:1], ALU.pow))
        L.e_p2.w(POOL)
        L.e_p3.s(POOL.tensor_scalar(L.sm[:, sl, 2:3], L.mv[:, sl, 0:1], L.sm[:, sl, 1:2], -1.0, ALU.mult, ALU.mult))
        L.p3_cnt[i] = L.e_p3.n
        L.e_p3.w(ACT)
        if i - 2 in L.ho_cnt:
            L.e_ho.w(ACT, L.ho_cnt[i - 2])
        L.e_nrm.s(ACT.activation(L.nrm[:, sl, :], pre, AF.Identity, bias=L.sm[:, sl, 2:3], scale=L.sm[:, sl, 1:2]))
        L.nrm_cnt[i] = L.e_nrm.n
        return i

    def ln_back(i, t, dst_dram, write_hT=True, tr_banks=(6, 7)):
        sl = i % 2
        L.e_nrm.w(DVE, L.nrm_cnt[i])
        L.e_mg.s(DVE.tensor_tensor(L.nrm[:, sl, :], L.nrm[:, sl, :], L.g[:], ALU.mult))
        L.e_mg.w(DVE)
        if i - 2 in L.st_cnt:
            L.e_st.w(DVE, L.st_cnt[i - 2])
            L.e_tr.w(DVE, L.tr_cnt[i - 2])
        for (e_, c_) in L.extra.get(i - 2, []):
            e_.w(DVE, c_)
        L.e_ho.s(DVE.tensor_tensor(L.ho[:, sl, :], L.nrm[:, sl, :], L.b[:], ALU.add))
        L.ho_cnt[i] = L.e_ho.n
        L.e_ho.w(SP)
        L.e_st.s(SP.dma_start(out=dst_dram[t * 128:(t + 1) * 128, :], in_=L.ho[:, sl, :]), 16)
        L.st_cnt[i] = L.e_st.n
        if write_hT:
            ln_transposes(L.ho[:, sl, :], t, tr_banks)
        L.tr_cnt[i] = L.e_tr.n
        return sl

    def ln_transposes(src, t, tr_banks, src_ready=None):
        if src_ready is None:
            L.e_ho.w(PE)
        else:
            src_ready[0].w(PE, src_ready[1])
        if len(tr_banks) == 2:
            L.e_cp.w(PE)
            for c in range(8):
                bk = tr_banks[c // 4]
                ins = PE.transpose(PS[:, bk, (c % 4) * 128:(c % 4 + 1) * 128], src[:, c * 128:(c + 1) * 128], ident[:])
            L.e_tr.s(ins)
            L.e_tr.w(DVE)
            for hb in range(2):
                ins = DVE.tensor_copy(hT[:, hb * 4:(hb + 1) * 4, 1 + t * 128:1 + (t + 1) * 128],
                                      PS[:, tr_banks[hb], :].rearrange("p (c n) -> p c n", c=4))
            L.e_cp.s(ins)
        else:
            bk = tr_banks[0]
            for hb in range(2):
                L.e_cp.w(PE)
                for c4 in range(4):
                    c = hb * 4 + c4
                    ins = PE.transpose(PS[:, bk, c4 * 128:(c4 + 1) * 128], src[:, c * 128:(c + 1) * 128], ident[:])
                L.e_tr.s(ins)
                L.e_tr.w(DVE)
                L.e_cp.s(DVE.tensor_copy(hT[:, hb * 4:(hb + 1) * 4, 1 + t * 128:1 + (t + 1) * 128],
                                         PS[:, bk, :].rearrange("p (c n) -> p c n", c=4)))

    class WS:
        def __init__(self):
            self.dsem = [ev(f"wsd{k}") for k in range(4)]
            self.cast = ev("wscast")
            self.i = 0
            self.pend = []

        def setup(self, st, nstage=3, free=1024, defer=1):
            assert not self.pend
            self.n = nstage
            self.defer = defer
            self.stg = sbuf(st, "wstg", [128, nstage, free], F32)

        def _emit_cast(self):
            (k, need, dst, dst_free, stg) = self.pend.pop(0)
            self.dsem[k].w(ACT, need)
            if dst_free is not None:
                for (e_, c_) in (dst_free() if callable(dst_free) else [dst_free]):
                    e_.w(ACT, c_)
            sh = dst.shape
            sz = 1
            for v in sh[1:]:
                sz *= v
            sview = stg[:, k, 0:sz]
            if len(sh) == 3:
                sview = sview.rearrange("p (a b) -> p a b", a=sh[1])
            self.cast.s(ACT.copy(dst, sview))

        def flush(self):
            while self.pend:
                self._emit_cast()

        def fetch(self, srcs, dst, dst_free=None, gate=None):
            k = self.i % self.n
            self.i += 1
            if gate is not None:
                gate.w(SP, 1)
            if self.i - self.n > 0:
                self.cast.w(SP, self.i - self.n)
            for (src, view) in srcs:
                self.dsem[k].s(SP.dma_start(out=view(self.stg[:, k, :]), in_=src), 16)
            self.pend.append((k, self.dsem[k].n, dst, dst_free, self.stg))
            while len(self.pend) > self.defer:
                self._emit_cast()
            return self.i

    ws = WS()

    def v3(a, b, c0=0, cw=None):
        def f(sl):
            v = sl[:, 0:a * b].rearrange("p (a b) -> p a b", a=a)
            if cw is not None:
                v = v[:, :, c0:c0 + cw]
            return v
        return f

    def dbg_dump(st_list, src_ap, shape, cast=True):
        de = ev("dbg")
        with ExitStack() as s2:
            if cast:
                tmpf = sbuf(s2, "dbgf", shape, F32)
                de.s(DVE.tensor_copy(tmpf[:], src_ap))
                de.w(SP)
                de.s(SP.dma_start(out=dbg_d.ap(), in_=tmpf[:]), 16)
            else:
                de.s(SP.dma_start(out=dbg_d.ap(), in_=src_ap), 16)
            de.w(SP)
        for s_ in st_list:
            s_.close()
        ES.close()
        return nc

    with ExitStack() as st:
        ln_setup(st, lnin_g_d[0:1, :], lnin_b_d[0:1, :])
        xt = sbuf(st, "xt", [128, 2, D], F32)
        xs = [ev("xl0"), ev("xl1")]
        for t in range(NT):
            sl = t % 2
            L.e_nrm.w(SP, L.e_nrm.n - 1)
            xs[sl].s(SP.dma_start(out=xt[:, sl, :], in_=x_d[t * 128:(t + 1) * 128, :]), 16)
            i_ln = ln_front(xt[:, sl, :], (xs[sl], xs[sl].n))
            if t >= 1:
                ln_back(i_ln - 1, t - 1, hres_d)
        ln_back(i_ln, NT - 1, hres_d)
        L.e_st.w(SP)
        barrier()
        if dbg is not None and dbg[0] == "hT0":
            return dbg_dump([st], hT[:], [128, 8, T + 2])

    class PRing:
        def __init__(self, banks):
            self.banks = banks
            self.i = 0
            self.rel = {}

        def acquire(self):
            i = self.i
            self.i += 1
            j = i - len(self.banks)
            if j >= 0:
                for (e_, c_) in self.rel[j]:
                    e_.w(PE, c_)
            return i, self.banks[i % len(self.banks)]

        def release(self, i, *pairs):
            self.rel[i] = list(pairs)

    e_pe = ev("pe")
    e_xs, e_dv, e_pb, e_out = ev("xs"), ev("dv"), ev("pb"), ev("out")
    e_vc = ev("vc")
    e_ac, e_u, e_ue, e_c, e_y = ev("ac"), ev("u"), ev("ue"), ev("c"), ev("y")
    e_prm = ev("prm")
    e_go = ev("go")
    e_ms = ev("ms")
    e_sg, e_pr, e_s01, e_mgd, e_pre, e_fh = ev("sg"), ev("pr"), ev("s01"), ev("mgd"), ev("pre"), ev("fh")
    e_gu, e_gl, e_g, e_hd = ev("gu"), ev("gl"), ev("g"), ev("hd")
    hrs = [ev("hr0"), ev("hr1")]
    e_wq = ev("wq")
    wul = [ev("wul0"), ev("wul1")]

    def stage_dump(name, stacks, src, shape, cast=True):
        if dbg is not None and dbg[0] == name:
            barrier()
            return dbg_dump(stacks, src, shape, cast)
        return None

    e_fx = ev("fx")
    e_ws, e_wexp, e_wm, e_wpv, e_wo, e_w1, e_wy, e_wtr, e_tcp = (ev("ws"), ev("wexp"), ev("wm"), ev("wpv"), ev("wo"),
                                                                  ev("w1"), ev("wy"), ev("wtr"), ev("tcp"))
    kvs = [ev("kv0"), ev("kv1")]
    e_qk, e_exp, e_pv, e_oev, e_n, e_np, e_yb, e_ytr, e_ycp = (ev("qk"), ev("exp"), ev("pv"), ev("oev"), ev("n"),
                                                                ev("np"), ev("yb"), ev("ytr"), ev("ycp"))
    for l in range(DEPTH):
        lam_init = 0.8 - 0.6 * math.exp(-0.3 * l)
        win_f, win_gate = WF[("w_in", l)][2], WF[("w_in", l)][3]
        LY = ExitStack()
        MIX = ExitStack()
        convw = sbuf(LY, f"convw{l}", [128, 12], F32)
        fcw = sbuf(LY, f"fcw{l}", [128, 132], F32)
        bgt = sbuf(LY, f"bgt{l}", [128, 24], F32)
        neglam = sbuf(LY, f"neglam{l}", [128, 1], F32)
        gsub = sbuf(LY, f"gsub{l}", [128, 128], F32)
        esink = sbuf(LY, f"esink{l}", [128, 8], F32)
        uedge = sbuf(LY, f"uedge{l}", [128, 8], F32)
        abe = sbuf(LY, f"abe{l}", [128, 8], F32)
        c3e = sbuf(LY, f"c3e{l}", [128, 8], F32)
        TMP = ExitStack()
        cwn = sbuf(TMP, f"cwn{l}", [3, 512], F32)
        fcn = sbuf(TMP, f"fcn{l}", [3, 2 * D_FF], F32)
        bgn = sbuf(TMP, f"bgn{l}", [24, 128], F32)
        lamt = sbuf(TMP, f"lamt{l}", [128, 256], F32)
        lamp = sbuf(TMP, f"lamp{l}", [128, 128], F32)
        lams = sbuf(TMP, f"lams{l}", [128, 4], F32)
        load_wait([(cwn[:], conv_w_d[l]), (fcn[:], fcw_d[l]), (bgn[:], bg_d[l].rearrange("(a p) -> a p", p=128)),
                   (lamt[:], dlam_d[l:l + 1, :].partition_broadcast(128)),
                   (gsub[:], subg_d[l:l + 1, :].partition_broadcast(128)),
                   (esink[:], sink_d[l:l + 1, :].partition_broadcast(128))], [PE, DVE, ACT, POOL])
        for cc in range(4):
            PE.transpose(PS[:, 7, cc * 3:(cc + 1) * 3], cwn[0:3, cc * 128:(cc + 1) * 128], ident[0:3, 0:3])
        for m in range(44):
            PE.transpose(PS[:, 7, 12 + m * 3:12 + (m + 1) * 3], fcn[0:3, m * 128:(m + 1) * 128], ident[0:3, 0:3])
        e_prm.s(PE.transpose(PS[:, 7, 144:168], bgn[0:24, :], ident[0:24, 0:24]))
        e_prm.w(DVE)
        DVE.tensor_copy(convw[:], PS[:, 7, 0:12])
        DVE.tensor_copy(fcw[:], PS[:, 7, 12:144])
        DVE.tensor_copy(bgt[:], PS[:, 7, 144:168])
        DVE.tensor_tensor(lamp[:, 0:64], lamt[:, 0:64], lamt[:, 64:128], ALU.mult)
        e_prm.s(DVE.tensor_tensor(lamp[:, 64:128], lamt[:, 128:192], lamt[:, 192:256], ALU.mult))
        e_prm.w(DVE)
        e_prm.s(DVE.tensor_reduce(lams[:, 0:2], lamp[:].rearrange("p (a b) -> p a b", a=2), AX.X, ALU.add))
        e_prm.w(ACT)
        ACT.activation(esink[:], esink[:], AF.Exp)
        e_prm.s(ACT.activation(lams[:, 2:4], lams[:, 0:2], AF.Exp))
        e_prm.w(DVE)
        e_prm.s(DVE.tensor_tensor(neglam[:], lams[:, 3:4], lams[:, 2:3], ALU.subtract))
        e_prm.w(DVE)
        DVE.tensor_scalar(neglam[:], neglam[:], -lam_init, None, ALU.add)
        e_prm.s(DVE.tensor_scalar(gsub[:], gsub[:], 1.0 - lam_init, None, ALU.mult))
        for e in ENGS:
            e_prm.w(e)
        barrier()
        TMP.close()
        yT = sbuf(MIX, f"yT{l}", [128, 12 * T + 512], BF16)
        yaT = yT[:, 0:4 * T].rearrange("p (c t) -> p c t", c=4)
        ybT = yT[:, 4 * T:8 * T].rearrange("p (c t) -> p c t", c=4)
        ycT = yT[:, 8 * T:12 * T].rearrange("p (c t) -> p c t", c=4)
        Kown = ybT
        Vown = yT[:, 8 * T:8 * T + 4 * 16 * 129].rearrange("p (h t e) -> p h t e", h=4, t=16)
        QS = ExitStack()
        QT = sbuf(QS, f"QT{l}", [128, 4, T], BF16)
        WIN = ExitStack()
        SQT = sbuf(WIN, f"SQT{l}", [128, 4, T], BF16)
        SKx = sbuf(WIN, f"SKx{l}", [128, 18 * 128], BF16)
        SVx = sbuf(WIN, f"SVx{l}", [128, 18, 2, 65], BF16)

        PA = ExitStack()
        ws.setup(PA, 3, 1024)
        NWS = 4
        Wsl = sbuf(PA, f"Wsl{l}", [128, NWS, 8, 128], BF16)
        ubuf = sbuf(PA, f"ubuf{l}", [128, T + 2], F32)
        Xs = sbuf(PA, f"Xs{l}", [128, 2, 512], F32)
        Ab = sbuf(PA, f"Ab{l}", [128, 2, 512], F32)
        Bb = sbuf(PA, f"Bb{l}", [128, 2, 512], F32)
        tmpc = sbuf(PA, f"tmpc{l}", [128, 2, 512], F32)
        cb = sbuf(PA, f"cb{l}", [128, 3, 512], F32)
        e_ms.s(POOL.memset(ubuf[:], 0.0))
        e_ms.s(POOL.memset(yT[:, 8 * T:12 * T + 512], 1.0))
        e_ms.s(POOL.memset(SVx[:], 1.0))
        for e in (DVE, ACT, PE):
            e_ms.w(e)

        pr = PRing([0, 1, 2, 3, 4, 5])
        wblk = [0]
        wrel = {}

        def fetch_in(col_segs):
            bi = wblk[0]
            wblk[0] += 1
            slot = bi % NWS
            srcs = []
            for (c0_, ncol, d0) in col_segs:
                srcs.append((win_f[:, c0_:c0_ + ncol].rearrange("(k p) c -> p k c", p=128), v3(8, 128, d0, ncol)))
            dfree = None
            if bi >= NWS:
                dfree = (e_pe, wrel[bi - NWS])
            cnt = ws.fetch(srcs, Wsl[:, slot, :, :], dst_free=dfree, gate=win_gate)
            return bi, slot, cnt

        def rope_block(blk, dst_fn):
            bi, slot, cnt = blk
            for tc in range(4):
                i, bk = pr.acquire()
                if tc == 0:
                    PE.wait_ge(ws.cast.sem, cnt)
                for k in range(8):
                    ins = PE.matmul(PS[:, bk, :], Wsl[:, slot, k, :], hT[:, k, 1 + tc * 512:1 + (tc + 1) * 512],
                                    start=(k == 0), stop=(k == 7))
                e_pe.s(ins)
                sl = e_xs.n % 2
                e_pe.w(ACT)
                e_dv.w(ACT, e_dv.n - 1)
                e_xs.s(ACT.copy(Xs[:, sl, :], PS[:, bk, :]))
                pr.release(i, (e_xs, e_xs.n))
                cs = slice(tc * 512, (tc + 1) * 512)
                e_xs.w(DVE)
                e_out.w(DVE, e_out.n - 1)
                DVE.tensor_tensor(Ab[:, sl, :], Xs[:, sl, :], CC[:, cs], ALU.mult)
                DVE.tensor_tensor(Bb[0:32, sl, :], Xs[32:64, sl, :], TS[32:64, cs], ALU.mult)
                DVE.tensor_tensor(Bb[32:64, sl, :], Xs[0:32, sl, :], TS[0:32, cs], ALU.mult)
                DVE.tensor_tensor(Bb[64:96, sl, :], Xs[96:128, sl, :], TS[96:128, cs], ALU.mult)
                e_dv.s(DVE.tensor_tensor(Bb[96:128, sl, :], Xs[64:96, sl, :], TS[64:96, cs], ALU.mult))
                e_dv.w(POOL)
                e_out.s(POOL.tensor_tensor(dst_fn(tc), Ab[:, sl, :], Bb[:, sl, :], ALU.add))
            wrel[bi] = e_pe.n

        def v_block(blk, dst_fn, view):
            bi, slot, cnt = blk
            for tg in range(4):
                i, bk = pr.acquire()
                if tg == 0:
                    PE.wait_ge(ws.cast.sem, cnt)
                for t4 in range(4):
                    t = tg * 4 + t4
                    for k in range(8):
                        ins = PE.matmul(PS[:, bk, t4 * 128:(t4 + 1) * 128], hT[:, k, 1 + t * 128:1 + (t + 1) * 128],
                                        Wsl[:, slot, k, :], start=(k == 0), stop=(k == 7), skip_group_check=True)
                e_pe.s(ins)
                e_pe.w(ACT)
                e_vc.s(ACT.copy(dst_fn(tg), view(PS[:, bk, :])))
                pr.release(i, (e_vc, e_vc.n))
            wrel[bi] = e_pe.n

        cc_k, cc_v, cc_e, cc_u = ev("cck"), ev("ccv"), ev("cce"), ev("ccu")

        def act_gk():
            e_out.w(SP)
            e_go.s(SP.dma_start(out=gk_in[l].ap().rearrange("(h p) t -> p h t", p=128), in_=Kown), 16)

        def act_gv():
            e_vc.w(SP)
            e_out.w(SP)
            e_go.s(SP.dma_start(out=gv_in[l].ap().rearrange("(h p) f -> p h f", p=128),
                                in_=yT[:, 8 * T:8 * T + 4 * 16 * 129].rearrange("p (h f) -> p h f", h=4)), 16)
            e_go.s(SP.dma_start(out=ge_in[l][:, 0:128], in_=SKx[:, 128:256]), 16)
            e_go.s(SP.dma_start(out=ge_in[l][:, 128:256], in_=SKx[:, 16 * 128:17 * 128]), 16)
            e_go.s(SP.dma_start(out=ge_in[l][:, 256:386], in_=SVx[:, 1, :, :].rearrange("p g d -> p (g d)")), 16)
            e_go.s(SP.dma_start(out=ge_in[l][:, 386:516], in_=SVx[:, 16, :, :].rearrange("p g d -> p (g d)")), 16)
            e_go.w(POOL)
            for (cev, gi, go_) in ((cc_k, gk_in[l], gk_out[l]), (cc_v, gv_in[l], gv_out[l]), (cc_e, ge_in[l], ge_out[l])):
                POOL.collective_compute("AllGather", ALU.bypass, replica_groups=RG,
                                        ins=[gi.ap().opt()], outs=[go_.ap().opt()]).then_inc(cev.sem)
                cev.n = 1

        def act_gu():
            e_ue.w(SP)
            e_go.s(SP.dma_start(out=gu_in[l][:, :], in_=uedge[:]), 16)
            e_go.w(POOL)
            POOL.collective_compute("AllGather", ALU.bypass, replica_groups=RG,
                                    ins=[gu_in[l].ap().opt()], outs=[gu_out[l].ap().opt()]).then_inc(cc_u.sem)
            cc_u.n = 1

        axst = {}

        def proc_ax(blk, cc):
            axst[cc] = blk

        def proc_ac(blk, cc):
            b_ax, b_ac = axst[cc], blk
            for tc in range(4):
                i1, bk1 = pr.acquire()
                if tc == 0:
                    PE.wait_ge(ws.cast.sem, b_ac[2])
                for k in range(8):
                    ins = PE.matmul(PS[:, bk1, :], Wsl[:, b_ax[1], k, :], hT[:, k, 1 + tc * 512:1 + (tc + 1) * 512],
                                    start=(k == 0), stop=(k == 7))
                i2, bk2 = pr.acquire()
                for k in range(8):
                    ins = PE.matmul(PS[:, bk2, :], Wsl[:, b_ac[1], k, :], hT[:, k, 1 + tc * 512:1 + (tc + 1) * 512],
                                    start=(k == 0), stop=(k == 7))
                e_pe.s(ins)
                sl = e_ac.n % 2
                e_pe.w(ACT)
                e_u.w(ACT, e_u.n - 1)
                e_ac.s(ACT.copy(tmpc[:, sl, :], PS[:, bk2, :]))
                pr.release(i2, (e_ac, e_ac.n))
                e_ac.w(DVE)
                e_u.s(DVE.tensor_tensor(ubuf[:, 1 + tc * 512:1 + (tc + 1) * 512], PS[:, bk1, :], tmpc[:, sl, :], ALU.mult))
                pr.release(i1, (e_u, e_u.n))
            wrel[b_ax[0]] = e_pe.n
            wrel[b_ac[0]] = e_pe.n
            e_u.w(POOL)
            POOL.tensor_copy(uedge[:, 2 * cc:2 * cc + 1], ubuf[:, 1:2])
            e_ue.s(POOL.tensor_copy(uedge[:, 2 * cc + 1:2 * cc + 2], ubuf[:, T:T + 1]))

        def proc_ab(blk, cc):
            b_ab = blk
            for tc in range(4):
                i, bk = pr.acquire()
                if tc == 0:
                    PE.wait_ge(ws.cast.sem, b_ab[2])
                for k in range(8):
                    ins = PE.matmul(PS[:, bk, :], Wsl[:, b_ab[1], k, :], hT[:, k, 1 + tc * 512:1 + (tc + 1) * 512],
                                    start=(k == 0), stop=(k == 7))
                e_pe.s(ins)
                t0 = tc * 512
                e_u.w(DVE)
                e_y.w(DVE)
                e_c.s(DVE.tensor_scalar(cb[:, 0, :], ubuf[:, 1 + t0:513 + t0], convw[:, cc * 3 + 1:cc * 3 + 2], None, ALU.mult))
                e_c.w(DVE)
                e_c.s(DVE.scalar_tensor_tensor(cb[:, 1, :], ubuf[:, t0:512 + t0], convw[:, cc * 3:cc * 3 + 1], cb[:, 0, :],
                                               ALU.mult, ALU.add))
                e_c.w(DVE)
                e_c.s(DVE.scalar_tensor_tensor(cb[:, 2, :], ubuf[:, 2 + t0:514 + t0], convw[:, cc * 3 + 2:cc * 3 + 3], cb[:, 1, :],
                                               ALU.mult, ALU.add))
                e_c.w(DVE)
                e_pe.w(DVE)
                if tc == 0:
                    DVE.tensor_copy(abe[:, 2 * cc:2 * cc + 1], PS[:, bk, 0:1])
                    DVE.tensor_copy(c3e[:, 2 * cc:2 * cc + 1], cb[:, 2, 0:1])
                if tc == 3:
                    DVE.tensor_copy(abe[:, 2 * cc + 1:2 * cc + 2], PS[:, bk, 511:512])
                    DVE.tensor_copy(c3e[:, 2 * cc + 1:2 * cc + 2], cb[:, 2, 511:512])
                e_y.s(DVE.tensor_tensor(yaT[:, cc, t0:t0 + 512], PS[:, bk, :], cb[:, 2, :], ALU.mult))
                pr.release(i, (e_y, e_y.n))
            wrel[b_ab[0]] = e_pe.n

        sched = []
        for h in range(4):
            sched.append(([(O_DK + h * 128, 128, 0)], lambda blk, h=h: rope_block(blk, lambda tc: Kown[:, h, tc * 512:(tc + 1) * 512])))
        sched.append(([(O_SK, 128, 0)], lambda blk: rope_block(blk, lambda tc: SKx[:, 128 + tc * 512:128 + (tc + 1) * 512])))
        sched.append((None, act_gk))
        for h in range(4):
            sched.append(([(O_DV + h * 128, 128, 0)],
                          lambda blk, h=h: v_block(blk, lambda tg: Vown[:, h, tg * 4:(tg + 1) * 4, 0:128],
                                                   lambda p: p.rearrange("p (a b) -> p a b", a=4))))
        sched.append(([(O_SV, 128, 0)],
                      lambda blk: v_block(blk, lambda tg: SVx[:, 1 + tg * 4:1 + (tg + 1) * 4, :, 0:64],
                                          lambda p: p.rearrange("p (a g d) -> p a g d", a=4, g=2))))
        sched.append((None, act_gv))
        for cc in range(4):
            sched.append(([(O_AX + cc * 128, 128, 0)], lambda blk, cc=cc: proc_ax(blk, cc)))
            sched.append(([(O_AC + cc * 128, 128, 0)], lambda blk, cc=cc: proc_ac(blk, cc)))
            sched.append(([(O_AB + cc * 128, 128, 0)], lambda blk, cc=cc: proc_ab(blk, cc)))
        sched.append((None, act_gu))
        for i_ in range(4):
            sched.append(([(O_SQ + i_ * 64, 64, 0), (O_SQ + (4 + i_) * 64, 64, 64)],
                          lambda blk, i_=i_: rope_block(blk, lambda tc: SQT[:, i_, tc * 512:(tc + 1) * 512])))
        for h in range(4):
            sched.append(([(O_DQ + h * 128, 128, 0)], lambda blk, h=h: rope_block(blk, lambda tc: QT[:, h, tc * 512:(tc + 1) * 512])))
        wlist = [e_ for e_ in sched if e_[0] is not None]
        fetched = []

        def try_fetch():
            while len(fetched) < len(wlist):
                bi = len(fetched)
                if bi >= NWS and (bi - NWS) not in wrel:
                    break
                fetched.append(fetch_in(wlist[bi][0]))
            if len(fetched) == len(wlist):
                ws.flush()

        try_fetch()
        wi = 0
        for (segs, fn) in sched:
            if segs is None:
                fn()
            else:
                fn(fetched[wi])
                wi += 1
                try_fetch()
        barrier()
        r_ = stage_dump(f"K{l}", [PA, WIN, QS, MIX, LY], Kown, [128, 4, T], False)
        if r_ is not None:
            return r_
        r_ = stage_dump(f"Q{l}", [PA, WIN, QS, MIX, LY], QT[:], [128, 4, T], False)
        if r_ is not None:
            return r_
        r_ = stage_dump(f"YA{l}", [PA, WIN, QS, MIX, LY], yaT, [128, 4, T], False)
        if r_ is not None:
            return r_
        r_ = stage_dump(f"SK{l}", [PA, WIN, QS, MIX, LY], SKx[:], [128, 18 * 128], False)
        if r_ is not None:
            return r_
        PA.close()
        WB = ExitStack()
        guS = sbuf(WB, f"guS{l}", [128, 8, 8], F32)
        gut = sbuf(WB, f"gut{l}", [128, 2, 8, 8], F32)
        guh = sbuf(WB, f"guh{l}", [128, 2, 8], F32)
        yfx = sbuf(WB, f"yfx{l}", [128, 3, 8], F32)
        geS = sbuf(WB, f"geS{l}", [128, 8, 516], BF16)
        tmpS = sbuf(WB, f"tmpS{l}", [128, 8, 516], F32)
        red = sbuf(WB, f"red{l}", [128, 2, 516], F32)
        PTw = sbuf(WB, f"PTw{l}", [128, 2, 3, 512], BF16)
        Owsb = sbuf(WB, f"Owsb{l}", [128, 2, 4, 65], F32)
        lw = sbuf(WB, f"lw{l}", [128, 2, 2, 4], F32)
        ycw = sbuf(WB, f"ycw{l}", [128, 2, 256], F32)
        cc_u.w(SP, 1)
        cc_e.w(SP, 1)
        load_wait([(guS[:], gu_out[l].ap().rearrange("(r p) c -> p r c", p=128)),
                   (geS[:], ge_out[l].ap().rearrange("(r p) c -> p r c", p=128))], [DVE])
        for side in range(2):
            for r in range(8):
                ins = DVE.tensor_scalar(gut[:, side, r, :], guS[:, r, :], selv[:, side * 8 + r:side * 8 + r + 1], None, ALU.mult)
        e_fx.s(ins)
        e_fx.w(DVE)
        for side in range(2):
            ins = DVE.tensor_reduce(guh[:, side, :], gut[:, side, :, :].rearrange("p r c -> p c r"), AX.X, ALU.add)
        e_fx.s(ins)
        e_fx.w(DVE)
        cw3 = convw[:].rearrange("p (c j) -> p c j", j=3)
        gh3 = guh[:].rearrange("p s (c e) -> p s c e", e=2)
        c33 = c3e[:].rearrange("p (c e) -> p c e", e=2)
        ab3 = abe[:].rearrange("p (c e) -> p c e", e=2)
        yf3 = yfx[:].rearrange("p a (c e) -> p a c e", e=2)
        DVE.tensor_tensor(yf3[:, 0, :, 0], cw3[:, :, 0], gh3[:, 0, :, 1], ALU.mult)
        e_fx.s(DVE.tensor_tensor(yf3[:, 0, :, 1], cw3[:, :, 2], gh3[:, 1, :, 0], ALU.mult))
        e_fx.w(DVE)
        e_fx.s(DVE.tensor_tensor(yfx[:, 1, :], yfx[:, 0, :], c3e[:], ALU.add))
        e_fx.w(DVE)
        e_fx.s(DVE.tensor_tensor(yfx[:, 2, :], yfx[:, 1, :], abe[:], ALU.mult))
        e_fx.w(DVE)
        yf2 = yfx[:, 2, :].rearrange("p (c e) -> p c e", e=2)
        DVE.tensor_copy(yaT[:, :, 0], yf2[:, :, 0])
        e_fx.s(DVE.tensor_copy(yaT[:, :, T - 1], yf2[:, :, 1]))

        for side in range(2):
            for r in range(8):
                ins = DVE.tensor_scalar(tmpS[:, r, :], geS[:, r, :], selv[:, side * 8 + r:side * 8 + r + 1], None, ALU.mult)
            e_fx.s(ins)
            e_fx.w(DVE)
            e_fx.s(DVE.tensor_reduce(red[:, side, :], tmpS[:].rearrange("p r c -> p c r"), AX.X, ALU.add))
            e_fx.w(DVE)
        DVE.tensor_copy(SKx[:, 0:128], red[:, 0, 128:256])
        DVE.tensor_copy(SKx[:, 17 * 128:18 * 128], red[:, 1, 0:128])
        DVE.tensor_copy(SVx[:, 0, :, :].rearrange("p g d -> p (g d)"), red[:, 0, 386:516])
        e_fx.s(DVE.tensor_copy(SVx[:, 17, :, :].rearrange("p g d -> p (g d)"), red[:, 1, 256:386]))
        for e in (PE, ACT, POOL):
            e_fx.w(e)

        r_ = stage_dump(f"FX{l}", [WB, WIN, QS, MIX, LY], yaT, [128, 4, T], False)
        if r_ is not None:
            return r_
        prevtr = None
        n = 0
        for qb in range(NT):
            for g in range(2):
                sl = n % 2
                gs = slice(g * 64, (g + 1) * 64)
                e_wexp.w(PE, e_wexp.n - 1)
                for kk in range(3):
                    kb = qb + kk
                    ins = PE.matmul(PS[:, sl * 3 + kk, :], SKx[gs, kb * 128:(kb + 1) * 128],
                                    SQT[gs, :, qb * 128:(qb + 1) * 128], start=True, stop=True)
                e_ws.s(ins)
                e_ws.w(ACT)
                e_wpv.w(ACT, e_wpv.n - 1)
                e_wexp.s(ACT.activation(PTw[:, sl, :, :], PS[:, sl * 3:sl * 3 + 3, :], AF.Exp, scale=0.125))
                e_wexp.w(DVE)
                DVE.tensor_tensor(PTw[:, sl, 0, :], PTw[:, sl, 0, :], maskb[:, 0, :], ALU.mult)
                e_wm.s(DVE.tensor_tensor(PTw[:, sl, 2, :], PTw[:, sl, 2, :], maskb[:, 1, :], ALU.mult))
                e_wm.w(PE)
                e_wo.w(PE)
                first = True
                for i_ in range(4):
                    for kk in range(3):
                        kb = qb + kk
                        ins = PE.matmul(PS[:, 6, i_ * 65:(i_ + 1) * 65], PTw[:, sl, kk, i_ * 128:(i_ + 1) * 128],
                                        SVx[:, kb, g, :], start=first, stop=(kk == 2), skip_group_check=True)
                        first = False
                e_wpv.s(ins)
                if prevtr is not None:
                    prevtr()
                e_wpv.w(DVE)
                e_wtr.w(DVE, e_wtr.n - 1)
                e_wo.s(DVE.tensor_copy(Owsb[:, sl, :, :], PS[:, 6, 0:260].rearrange("p (i d) -> p i d", i=4)))
                e_wo.w(DVE)
                e_w1.s(DVE.tensor_tensor(lw[:, sl, 0, :], Owsb[:, sl, :, 64], esink[:, g * 4:(g + 1) * 4], ALU.add))
                e_w1.w(DVE)
                e_w1.s(DVE.reciprocal(lw[:, sl, 1, :], lw[:, sl, 0, :]))
                e_w1.w(DVE)
                for i_ in range(4):
                    ins = DVE.tensor_scalar(ycw[:, sl, i_ * 64:(i_ + 1) * 64], Owsb[:, sl, i_, 0:64], lw[:, sl, 1, i_:i_ + 1],
                                            None, ALU.mult)
                e_wy.s(ins)

                def tr(sl=sl, g=g, qb=qb, cnt=e_wy.n):
                    e_wy.w(PE, cnt)
                    e_tcp.w(PE)
                    for ii in range(2):
                        ins_ = PE.transpose(PS[:, 7, ii * 128:(ii + 1) * 128], ycw[:, sl, ii * 128:(ii + 1) * 128], ident[:])
                    e_wtr.s(ins_)
                    e_wtr.w(ACT)
                    e_tcp.s(ACT.copy(ycT[:, 2 * g:2 * g + 2, qb * 128:(qb + 1) * 128],
                                     PS[:, 7, 0:256].rearrange("p (a b) -> p a b", a=2)))
                prevtr = tr
                n += 1
        prevtr()
        barrier()
        r_ = stage_dump(f"YC{l}", [WB, WIN, QS, MIX, LY], yT[:, 0:12 * T], [128, 12 * T], False)
        if r_ is not None:
            return r_
        WB.close()
        WIN.close()

        DA = ExitStack()
        KVk = sbuf(DA, f"KVk{l}", [128, 2, T], BF16)
        KVv = sbuf(DA, f"KVv{l}", [128, 2, 16 * 129], BF16)
        PT = sbuf(DA, f"PT{l}", [128, 2, 1024], BF16)
        Osb = sbuf(DA, f"Osb{l}", [128, 9, 129], F32)
        rl = sbuf(DA, f"rl{l}", [128, 2, 8], F32)
        t1 = sbuf(DA, f"t1{l}", [128, 4, 128], F32)
        ob = sbuf(DA, f"ob{l}", [128, 4, 128], F32)
        junk = sbuf(DA, f"junk{l}", [128, 128], F32)
        ssb = sbuf(DA, f"ssb{l}", [128, 3, 4], F32)
        ybb = sbuf(DA, f"ybb{l}", [128, 4, 128], F32)
        cc_k.w(SP, 1)
        cc_v.w(SP, 1)
        if dbg is not None and dbg[0] == f"KV{l}":
            load_wait([(KVk[:, 0, :], gk_out[l][3 * 512 + 128:3 * 512 + 256, :]), (KVv[:, 0, :], gv_out[l][3 * 512 + 128:3 * 512 + 256, :])], [DVE])
            barrier()
            return dbg_dump([DA, QS, MIX, LY], KVk[:, 0, :], [128, T], False)
        blk_i = 0
        blk_end = {}
        pend_pv = None
        pend_tr = None
        stepc = 0

        def oacc(a):
            return PS[:, 4 + a // 3, (a % 3) * 129:(a % 3) * 129 + 129]

        hs_list = [(h_, s__) for h_ in range(4) for s__ in range(4)]
        if dbg is not None and len(dbg) > 3:
            hs_list = hs_list[:dbg[3]]
        for (h, s_) in hs_list:
            if True:
                for r in range(8 if (dbg is None or len(dbg) < 5) else dbg[4]):
                    slot = blk_i % 2
                    if blk_i >= 2:
                        e_pv.w(SP, blk_end[blk_i - 2])
                    row0 = r * 512 + h * 128
                    kvs[slot].s(SP.dma_start(out=KVk[:, slot, :], in_=gk_out[l][row0:row0 + 128, :]), 16)
                    kvs[slot].s(SP.dma_start(out=KVv[:, slot, :], in_=gv_out[l][row0:row0 + 128, :]), 16)
                    kv_need = kvs[slot].n
                    for kc in range(16):
                        sl = stepc % 2
                        first_hs = (r == 0 and kc == 0)
                        last_hs = (r == (7 if (dbg is None or len(dbg) < 5) else dbg[4] - 1) and kc == 15)
                        if kc == 0:
                            kvs[slot].w(PE, kv_need)
                        e_exp.w(PE, e_exp.n - 1)
                        PE.matmul(PS[:, 2 * sl, :], KVk[0:64, slot, kc * 128:(kc + 1) * 128], QT[0:64, h, s_ * 512:(s_ + 1) * 512],
                                  start=True, stop=True)
                        e_qk.s(PE.matmul(PS[:, 2 * sl + 1, :], KVk[64:128, slot, kc * 128:(kc + 1) * 128],
                                         QT[64:128, h, s_ * 512:(s_ + 1) * 512], start=True, stop=True))
                        e_qk.w(ACT)
                        e_pv.w(ACT)
                        e_exp.s(ACT.activation(PT[:, sl, :].rearrange("p (a b) -> p a b", a=2), PS[:, 2 * sl:2 * sl + 2, :], AF.Exp, scale=0.125))
                        if pend_pv is not None:
                            pend_pv()

                        def pv(sl=sl, slot=slot, kc=kc, first_hs=first_hs, last_hs=last_hs, cnt=e_exp.n, bi=blk_i):
                            e_exp.w(PE, cnt)
                            e_qk.w(PE)
                            if first_hs:
                                e_oev.w(PE)
                            for j in range(2):
                                for qt in range(4):
                                    a = j * 4 + qt
                                    ins_ = PE.matmul(oacc(a), PT[:, sl, j * 512 + qt * 128:j * 512 + (qt + 1) * 128],
                                                     KVv[:, slot, kc * 129:(kc + 1) * 129],
                                                     start=(first_hs and a % 3 == 0), stop=last_hs, skip_group_check=True)
                            e_pv.s(ins_)
                            if kc == 15:
                                blk_end[bi] = e_pv.n
                        pend_pv = pv
                        stepc += 1
                        if pend_tr is not None and r == 1 and kc == 0:
                            pend_tr()
                            pend_tr = None
                    blk_i += 1
                pend_pv()
                pend_pv = None
                e_pv.w(DVE)
                for b_ in range(3):
                    ins = DVE.tensor_copy(Osb[:, 3 * b_:3 * b_ + 3, :], PS[:, 4 + b_, 0:387].rearrange("p (a e) -> p a e", a=3))
                e_oev.s(ins)
                e_oev.w(DVE)
                if dbg is not None and dbg[0] == f"OS{l}":
                    barrier()
                    return dbg_dump([DA, QS, MIX, LY], Osb[:], [128, 9, 129], False)
                e_n.s(DVE.reciprocal(rl[:, 0, :], Osb[:, 0:8, 128]))
                e_n.w(DVE)
                e_n.s(DVE.tensor_scalar(rl[:, 1, 0:4], rl[:, 0, 4:8], neglam[:, 0:1], None, ALU.mult))
                e_n.w(DVE)
                e_ytr.w(DVE)
                for qt in range(4):
                    ins = DVE.tensor_scalar(t1[:, qt, :], Osb[:, 4 + qt, 0:128], rl[:, 1, qt:qt + 1], None, ALU.mult)
                e_n.s(ins)
                e_n.w(DVE)
                for qt in range(4):
                    ins = DVE.scalar_tensor_tensor(ob[:, qt, :], Osb[:, qt, 0:128], rl[:, 0, qt:qt + 1], t1[:, qt, :],
                                                   ALU.mult, ALU.add)
                e_n.s(ins)
                e_n.w(DVE)
                for qt in range(4):
                    ins = DVE.scalar_tensor_tensor(junk[:], ob[:, qt, :], 1.0, ob[:, qt, :], ALU.mult, ALU.mult,
                                                   accum_out=ssb[:, 0, qt:qt + 1])
                e_n.s(ins)
                e_n.w(POOL)
                e_np.s(POOL.tensor_scalar(ssb[:, 1, :], ssb[:, 0, :], 1.0 / 128.0, LN_EPS, ALU.mult, ALU.add))
                e_np.w(POOL)
                e_np.s(POOL.tensor_tensor(ssb[:, 2, :], ssb[:, 1, :], mhalf[:, 0:4], ALU.pow))
                e_np.w(DVE)
                for qt in range(4):
                    ins = DVE.scalar_tensor_tensor(ybb[:, qt, :], ob[:, qt, :], ssb[:, 2, qt:qt + 1], gsub[:], ALU.mult, ALU.mult)
                e_yb.s(ins)

                def trb(h=h, s_=s_, cnt=e_yb.n):
                    e_yb.w(PE, cnt)
                    e_ycp.w(PE)
                    for qt in range(4):
                        ins_ = PE.transpose(PS[:, 7, qt * 128:(qt + 1) * 128], ybb[:, qt, :], ident[:])
                    e_ytr.s(ins_)
                    e_ytr.w(DVE)
                    e_ycp.s(DVE.tensor_copy(ybT[:, h, s_ * 512:(s_ + 1) * 512], PS[:, 7, :]))
                pend_tr = trb
        pend_tr()
        barrier()
        r_ = stage_dump(f"Y{l}", [DA, QS, MIX, LY], yT[:, 0:12 * T], [128, 12 * T], False)
        if r_ is not None:
            return r_
        DA.close()
        QS.close()
        B3 = ExitStack()
        mergedT = sbuf(B3, f"mergedT{l}", [128, 8, T], BF16)
        B3A = ExitStack()
        ws.setup(B3A, 3, 1024)
        NG = 12
        Wg = sbuf(B3A, f"Wg{l}", [128, NG, 8, 128], BF16)
        sg = sbuf(B3A, f"sg{l}", [128, 2, 3, 512], F32)
        prd = sbuf(B3A, f"prd{l}", [128, 2, 3, 512], F32)
        s01 = sbuf(B3A, f"s01{l}", [128, 2, 512], F32)
        wgf, wg_gate = WF[("w_branch_gate", l)][2], WF[("w_branch_gate", l)][3]
        wbf, wb_gate = WF[("w_branch", l)][2], WF[("w_branch", l)][3]
        pr = PRing([0, 1, 2, 3, 4, 5])
        gblk = [0]
        grel = {}

        def fetch_g(kind, dc, n_):
            bi = gblk[0]
            gblk[0] += 1
            slot = bi % NG
            dfree = (e_pe, grel[bi - NG]) if bi >= NG else None
            if kind == 0:
                src = wgf[:, n_ * 1024 + dc * 128:n_ * 1024 + (dc + 1) * 128].rearrange("(k p) c -> p k c", p=128)
                cnt = ws.fetch([(src, v3(8, 128))], Wg[:, slot, :, :], dst_free=dfree, gate=wg_gate)
            else:
                src = wbf[n_ * 512:(n_ + 1) * 512, dc * 128:(dc + 1) * 128].rearrange("(k p) c -> p k c", p=128)
                cnt = ws.fetch([(src, v3(4, 128))], Wg[:, slot, 0:4, :], dst_free=dfree, gate=wb_gate)
            return bi, slot, cnt

        glist = [(kind, dc, n_) for dc in range(8) for n_ in range(3) for kind in (0, 1)]
        gfetched = []

        def try_fetch_g():
            while len(gfetched) < len(glist):
                bi = len(gfetched)
                if bi >= NG and (bi - NG) not in grel:
                    break
                gfetched.append(fetch_g(*glist[bi]))
            if len(gfetched) == len(glist):
                ws.flush()

        try_fetch_g()
        yTn = [yaT, ybT, ycT]
        step = 0
        pr_cnt = {}
        mg_cnt = {}
        for dc in range(8):
            blks = gfetched[dc * 6:(dc + 1) * 6]
            for tc in range(4):
                sl2 = step % 2
                for n_ in range(3):
                    bg_, bb_ = blks[2 * n_], blks[2 * n_ + 1]
                    i1, bk1 = pr.acquire()
                    if tc == 0:
                        PE.wait_ge(ws.cast.sem, bb_[2])
                    for k in range(8):
                        ins = PE.matmul(PS[:, bk1, :], Wg[:, bg_[1], k, :], hT[:, k, 1 + tc * 512:1 + (tc + 1) * 512],
                                        start=(k == 0), stop=(k == 7))
                    e_pe.s(ins)
                    c_gate = e_pe.n
                    i2, bk2 = pr.acquire()
                    for k in range(4):
                        ins = PE.matmul(PS[:, bk2, :], Wg[:, bb_[1], k, :], yTn[n_][:, k, tc * 512:(tc + 1) * 512],
                                        start=(k == 0), stop=(k == 3))
                    e_pe.s(ins)
                    c_br = e_pe.n
                    e_pe.w(ACT, c_gate)
                    if (step - 2, n_) in pr_cnt:
                        e_pr.w(ACT, pr_cnt[(step - 2, n_)])
                    e_sg.s(ACT.activation(sg[:, sl2, n_, :], PS[:, bk1, :], AF.Sigmoid,
                                          bias=bgt[:, n_ * 8 + dc:n_ * 8 + dc + 1], scale=1.0))
                    pr.release(i1, (e_sg, e_sg.n))
                    e_sg.w(DVE)
                    e_pe.w(DVE, c_br)
                    if (step - 2) in mg_cnt:
                        e_mgd.w(DVE, mg_cnt[step - 2])
                    e_pr.s(DVE.tensor_tensor(prd[:, sl2, n_, :], sg[:, sl2, n_, :], PS[:, bk2, :], ALU.mult))
                    pr_cnt[(step, n_)] = e_pr.n
                    pr.release(i2, (e_pr, e_pr.n))
                e_pr.w(POOL)
                e_s01.s(POOL.tensor_tensor(s01[:, sl2, :], prd[:, sl2, 0, :], prd[:, sl2, 1, :], ALU.add))
                e_s01.w(POOL)
                e_mgd.s(POOL.tensor_tensor(mergedT[:, dc, tc * 512:(tc + 1) * 512], s01[:, sl2, :], prd[:, sl2, 2, :], ALU.add))
                mg_cnt[step] = e_mgd.n
                step += 1
            for b_ in blks:
                grel[b_[0]] = e_pe.n
            try_fetch_g()
        barrier()
        B3A.close()

        B3B = ExitStack()
        ws.setup(B3B, 2, 1024)
        Wo = sbuf(B3B, f"Wo{l}", [128, 8, D], BF16)
        ln_setup(B3B, lnm_g_d[l:l + 1, :], lnm_b_d[l:l + 1, :])
        hr = sbuf(B3B, f"hr{l}", [128, 2, D], F32)
        pre = sbuf(B3B, f"pre{l}", [128, 2, D], F32)
        wof, wo_gate = WF[("w_o", l)][2], WF[("w_o", l)][3]
        for cb_ in range(8):
            src = wof[:, cb_ * 128:(cb_ + 1) * 128].rearrange("(k p) c -> p k c", p=128)
            wo_cnt = ws.fetch([(src, v3(8, 128))], Wo[:, :, cb_ * 128:(cb_ + 1) * 128], gate=wo_gate)
        ws.flush()
        PE.wait_ge(ws.cast.sem, wo_cnt)
        pr2 = PRing([0, 2, 4])
        cc_2 = ev("cc2")
        order = [0, NT - 1] + list(range(1, NT - 1))

        def b3_back(i_, t_):
            lsl = ln_back(i_, t_, hres_d)
            if t_ == 0:
                L.e_st.s(SP.dma_start(out=e2_in[l][0:1, :], in_=L.ho[0:1, lsl, :]), 16)
                L.st_cnt[i_] = L.e_st.n
            if t_ == NT - 1:
                L.e_st.s(SP.dma_start(out=e2_in[l][1:2, :], in_=L.ho[127:128, lsl, :]), 16)
                L.st_cnt[i_] = L.e_st.n
                L.e_st.w(POOL)
                POOL.collective_compute("AllGather", ALU.bypass, replica_groups=RG,
                                        ins=[e2_in[l].ap().opt()], outs=[e2_out[l].ap().opt()]).then_inc(cc_2.sem)
                cc_2.n = 1
        for ti, t in enumerate(order):
            sl = ti % 2
            e_pre.w(SP, e_pre.n - 1)
            hrs[sl].s(SP.dma_start(out=hr[:, sl, :], in_=hres_d[t * 128:(t + 1) * 128, :]), 16)
            hr_need = hrs[sl].n
            i, bk = pr2.acquire()
            for half in range(2):
                for k in range(8):
                    ins = PE.matmul(PS[:, bk + half, :], mergedT[:, k, t * 128:(t + 1) * 128], Wo[:, k, half * 512:(half + 1) * 512],
                                    start=(k == 0), stop=(k == 7))
            e_pe.s(ins)
            e_pe.w(DVE)
            hrs[sl].w(DVE, hr_need)
            L.e_nrm.w(DVE, L.e_nrm.n - 1)
            e_pre.s(DVE.scalar_tensor_tensor(pre[:, sl, :].rearrange("p (a b) -> p a b", a=2),
                                             hr[:, sl, :].rearrange("p (a b) -> p a b", a=2), ALPHA,
                                             PS[:, bk:bk + 2, :], ALU.mult, ALU.add))
            pr2.release(i, (e_pre, e_pre.n))
            i_ln = ln_front(pre[:, sl, :], (e_pre, e_pre.n))
            if ti >= 1:
                b3_back(i_ln - 1, order[ti - 1])
        b3_back(i_ln, order[-1])
        L.e_st.w(SP)
        barrier()
        r_ = stage_dump(f"HM{l}", [B3B, B3, MIX, LY], hT[:], [128, 8, T + 2], False)
        if r_ is not None:
            return r_
        B3B.close()
        B3.close()
        MIX.close()

        last = (l == DEPTH - 1)
        dst_d = out_d if last else hres_d
        with ExitStack() as st:
            e2S = sbuf(st, f"e2S{l}", [16, D], F32)
            e2b = sbuf(st, f"e2b{l}", [16, D], BF16)
            cc_2.w(SP, 1)
            load_wait([(e2S[:], e2_out[l][:, :])], [DVE])
            e_fh.s(DVE.tensor_copy(e2b[:], e2S[:]))
            e_fh.w(PE)
            for c in range(8):
                ins = PE.matmul(PS[:, 7, c * 2:(c + 1) * 2], e2b[0:16, c * 128:(c + 1) * 128], sel16b[0:16, 0:2], start=True, stop=True)
            e_fh.s(ins)
            e_fh.w(DVE)
            hv = PS[:, 7, 0:16].rearrange("p (c s) -> p c s", s=2)
            DVE.tensor_copy(hT[:, :, 0], hv[:, :, 0])
            e_fh.s(DVE.tensor_copy(hT[:, :, T + 1], hv[:, :, 1]))
            barrier()
        FF = ExitStack()
        Wd = sbuf(FF, f"Wd{l}", [128, NFF, D], BF16)
        gT = sbuf(FF, f"gT{l}", [128, NFF, 512], BF16)
        GU = sbuf(FF, f"GU{l}", [128, 2, 2, 514], F32)
        Hc = sbuf(FF, f"Hc{l}", [128, 2, 2, 512], F32)
        cbf = sbuf(FF, f"cbf{l}", [128, 2, 2, 512], F32)
        Wu = sbuf(FF, f"Wu{l}", [128, 4, 8, 128], BF16)
        hodef = sbuf(FF, f"hodef{l}", [128, D], F32)
        ws.setup(FF, 3, 1024, defer=2)
        ln_setup(FF, lnf_g_d[l:l + 1, :], lnf_b_d[l:l + 1, :])
        hr = sbuf(FF, f"hrf{l}", [128, 2, D], F32)
        pre = sbuf(FF, f"pref{l}", [128, 2, D], F32)
        wuf, wu_gate = WF[("w_ffn_up", l)][2], WF[("w_ffn_up", l)][3]
        wdf, wd_gate = WF[("w_ffn_down", l)][2], WF[("w_ffn_down", l)][3]
        for i_ in range(NFF):
            wd_cnt = ws.fetch([(wdf[i_ * 128:(i_ + 1) * 128, :], lambda s_: s_[:, 0:D])], Wd[:, i_, :], gate=wd_gate)
        ws.flush()
        ublk = [0]
        urel = {}

        wq_d = nc.dram_tensor(f"wq{l}", [2 * NFF, 128, 1024], BF16)
        store_cnt = {}

        def fetch_u(i_):
            bi = ublk[0]
            ublk[0] += 1
            sl0 = (bi % 2) * 2

            def dfree(bi=bi):
                return [(e_pe, urel[bi - 2]), (e_wq, store_cnt[bi - 2])] if bi >= 2 else []
            srcg = wuf[:, i_ * 128:(i_ + 1) * 128].rearrange("(k p) c -> p k c", p=128)
            srcu = wuf[:, D_FF + i_ * 128:D_FF + (i_ + 1) * 128].rearrange("(k p) c -> p k c", p=128)
            ws.fetch([(srcg, v3(8, 128))], Wu[:, sl0, :, :], dst_free=dfree, gate=wu_gate)
            cnt = ws.fetch([(srcu, v3(8, 128))], Wu[:, sl0 + 1, :, :], dst_free=dfree)
            return bi, sl0, ("C", cnt)

        def store_u(i_, blk):
            bi, sl0, (_, cnt) = blk
            ws.cast.w(SP, cnt)
            for wh in range(2):
                e_wq.s(SP.dma_start(out=wq_d[2 * i_ + wh].rearrange("p (k c) -> p k c", k=8), in_=Wu[:, sl0 + wh, :, :]), 16)
            store_cnt[bi] = e_wq.n

        def load_u(i_, first=False):
            bi = ublk[0]
            ublk[0] += 1
            sl0 = (bi % 2) * 2
            e_pe.w(SP, urel[bi - 2])
            if first:
                e_wq.w(SP)
            for wh in range(2):
                wul[bi % 2].s(SP.dma_start(out=Wu[:, sl0 + wh, :, :], in_=wq_d[2 * i_ + wh].rearrange("p (k c) -> p k c", k=8)), 16)
            return bi, sl0, ("L", wul[bi % 2].n)

        fcw3 = fcw[:].rearrange("p (m j) -> p m j", j=3)
        pdef = [None]

        def ffn_back(i_, t_, deferred):
            lsl = ln_back(i_, t_, dst_d, write_hT=(not last) and not deferred, tr_banks=(7,))
            if deferred:
                L.e_ho.w(POOL, L.ho_cnt[i_])
                e_hd.s(POOL.tensor_copy(hodef[:], L.ho[:, lsl, :]))
                cnt_hd = e_hd.n
                L.extra[i_] = [(e_hd, cnt_hd)]

                def dfn(t_=t_, cnt=cnt_hd):
                    ln_transposes(hodef[:], t_, (7,), src_ready=(e_hd, cnt))
                pdef[0] = dfn
        pru = PRing([0, 2, 4])
        prd2 = PRing([0, 2, 4])
        pend_def = None
        ustep = 0
        for q in range(4):
            c0 = q * 512
            if q == 0:
                ulist = [fetch_u(0), fetch_u(1)]
                ws.flush()
            else:
                ulist = [load_u(0, first=(q == 1)), load_u(1)]
            for i_ in range(NFF):
                bi, sl0, (kind_, cnt) = ulist[i_]
                gsl = ustep % 2
                i, bk = pru.acquire()
                if kind_ == "C":
                    PE.wait_ge(ws.cast.sem, cnt)
                else:
                    wul[bi % 2].w(PE, cnt)
                if q == 0 and i_ == 0:
                    e_fh.w(PE)
                for wh in range(2):
                    for k in range(8):
                        PE.matmul(PS[:, bk + wh, :], Wu[:, sl0 + wh, k, :], hT[:, k, c0:c0 + 512], start=(k == 0), stop=(k == 7))
                e_gu.w(PE)
                for wh in range(2):
                    for k in range(8):
                        ins = PE.matmul(PS[:, 6, wh * 2:wh * 2 + 2], Wu[:, sl0 + wh, k, :], hT[:, k, c0 + 512:c0 + 514],
                                        start=(k == 0 and wh == 0), stop=(k == 7), skip_group_check=True)
                e_pe.s(ins)
                urel[bi] = e_pe.n
                if q == 0:
                    store_u(i_, ulist[i_])
                if i_ + 2 < NFF:
                    ulist.append(fetch_u(i_ + 2) if q == 0 else load_u(i_ + 2))
                if q == 0 and i_ + 3 >= NFF:
                    ws.flush()
                if pdef[0] is not None and i_ == 2:
                    pdef[0]()
                    pdef[0] = None
                e_pe.w(ACT)
                e_c.w(ACT, e_c.n - 4)
                ACT.copy(GU[:, gsl, :, 0:512], PS[:, bk:bk + 2, :])
                e_gu.s(ACT.copy(GU[:, gsl, :, 512:514], PS[:, 6, 0:4].rearrange("p (w c) -> p w c", w=2)))
                pru.release(i, (e_gu, e_gu.n))
                e_gu.w(DVE)
                e_g.w(DVE, e_g.n - 1)
                ms = [i_, NFF + i_]
                for wh in range(2):
                    ins = DVE.tensor_scalar(cbf[:, 0, wh, :], GU[:, gsl, wh, 1:513], fcw3[:, ms[wh], 1:2], None, ALU.mult)
                e_c.s(ins)
                e_c.w(DVE)
                for wh in range(2):
                    ins = DVE.scalar_tensor_tensor(cbf[:, 1, wh, :], GU[:, gsl, wh, 0:512], fcw3[:, ms[wh], 0:1], cbf[:, 0, wh, :],
                                                   ALU.mult, ALU.add)
                e_c.s(ins)
                e_c.w(DVE)
                for wh in range(2):
                    ins = DVE.scalar_tensor_tensor(Hc[:, gsl, wh, :], GU[:, gsl, wh, 2:514], fcw3[:, ms[wh], 2:3], cbf[:, 1, wh, :],
                                                   ALU.mult, ALU.add)
                    e_c.s(ins)
                e_c.w(ACT, e_c.n - 1)
                e_gl.s(ACT.activation(Hc[:, gsl, 0, :], Hc[:, gsl, 0, :], AF.Gelu))
                e_gl.w(POOL)
                e_c.w(POOL)
                if i_ == 0:
                    e_pe.w(POOL)
                e_g.s(POOL.tensor_tensor(gT[:, i_, :], Hc[:, gsl, 0, :], Hc[:, gsl, 1, :], ALU.mult))
                ustep += 1
            e_g.w(PE)
            PE.wait_ge(ws.cast.sem, wd_cnt)
            for t4 in range(4):
                t = q * 4 + t4
                sl = e_pre.n % 2
                e_pre.w(SP, e_pre.n - 1)
                hrs[sl].s(SP.dma_start(out=hr[:, sl, :], in_=hres_d[t * 128:(t + 1) * 128, :]), 16)
                hr_need = hrs[sl].n
                i, bk = prd2.acquire()
                if t4 == 0:
                    for (e__, c__) in pru.rel.get(pru.i - 1, []) + pru.rel.get(pru.i - 2, []) + pru.rel.get(pru.i - 3, []):
                        e__.w(PE, c__)
                for half in range(2):
                    for i_ in range(NFF):
                        ins = PE.matmul(PS[:, bk + half, :], gT[:, i_, t4 * 128:(t4 + 1) * 128], Wd[:, i_, half * 512:(half + 1) * 512],
                                        start=(i_ == 0), stop=(i_ == NFF - 1))
                e_pe.s(ins)
                e_pe.w(DVE)
                hrs[sl].w(DVE, hr_need)
                L.e_nrm.w(DVE, L.e_nrm.n - 1)
                e_pre.s(DVE.scalar_tensor_tensor(pre[:, sl, :].rearrange("p (a b) -> p a b", a=2),
                                                 hr[:, sl, :].rearrange("p (a b) -> p a b", a=2), ALPHA,
                                                 PS[:, bk:bk + 2, :], ALU.mult, ALU.add))
                prd2.release(i, (e_pre, e_pre.n))
                i_ln = ln_front(pre[:, sl, :], (e_pre, e_pre.n))
                if t4 >= 1:
                    ffn_back(i_ln - 1, t - 1, False)
            ffn_back(i_ln, q * 4 + 3, (not last) and q < 3)
            for j_ in range(1, 4):
                for (e__, c__) in prd2.rel.get(prd2.i - j_, []):
                    e__.w(PE, c__)
        L.e_st.w(SP)
        barrier()
        FF.close()
        LY.close()

    ES.close()
    return nc


def _consts(c):
    cvec = np.zeros((128, 4), np.float32)
    p = np.arange(128)
    inv = 1.0 / (THETA ** (np.arange(0, 64, 2, dtype=np.float32) / np.float32(64)))
    cvec[:, 0] = inv.astype(np.float32)[p % 32]
    cvec[:, 1] = np.where((p % 64) < 32, 1.0, -1.0)
    cvec[:, 2] = 1.0 if c > 0 else 0.0
    cvec[:, 3] = 1.0 if c < NCORES - 1 else 0.0
    selv = np.zeros((128, 16), np.float32)
    if c > 0:
        selv[:, c - 1] = 1.0
    if c < NCORES - 1:
        selv[:, 8 + c + 1] = 1.0
    sel16 = np.zeros((16, 2), np.float32)
    if c > 0:
        sel16[(c - 1) * 2 + 1, 0] = 1.0
    if c < NCORES - 1:
        sel16[(c + 1) * 2 + 0, 1] = 1.0
    ident = np.eye(128, dtype=np.float32)
    kj = np.arange(128)[:, None]
    qi = np.arange(128)[None, :]
    mp = (kj >= qi).astype(np.float32)
    mn = (kj <= qi).astype(np.float32)
    masks = np.stack([np.tile(mp, (1, 4)), np.tile(mn, (1, 4))], axis=1).astype(np.float32)
    return cvec, selv, sel16, ident, masks


def _in_maps(inputs):
    f = lambda a: np.ascontiguousarray(np.asarray(a, dtype=np.float32))
    x = f(inputs["x"]).reshape(SEQ, D)
    pos = np.ascontiguousarray(np.asarray(inputs["positions"], dtype=np.int32)).reshape(SEQ)
    shared = {
        "ln_in_g": f(inputs["ln_in_g"]).reshape(1, D), "ln_in_b": f(inputs["ln_in_b"]).reshape(1, D),
        "conv_w": f(inputs["conv_w"]),
        "diff_lambda": f(inputs["diff_lambda"]).reshape(DEPTH, 256),
        "diff_subln_g": f(inputs["diff_subln_g"]), "swa_sink": f(inputs["swa_sink"]),
        "b_branch_gate": f(inputs["b_branch_gate"]),
        "ln_mix_g": f(inputs["ln_mix_g"]), "ln_mix_b": f(inputs["ln_mix_b"]),
        "ffn_conv_w": f(inputs["ffn_conv_w"]),
        "ln_ffn_g": f(inputs["ln_ffn_g"]), "ln_ffn_b": f(inputs["ln_ffn_b"]),
    }
    big = {}
    for (nm, R, C) in BIGW:
        w = f(inputs[nm]).reshape(DEPTH, R, C)
        for l in range(DEPTH):
            big[(nm, l)] = w[l]
    maps = []
    for c in range(NCORES):
        cvec, selv, sel16, ident, masks = _consts(c)
        m = dict(shared)
        for (nm, R, C) in BIGW:
            for l in range(DEPTH):
                rs = R // NCORES
                m[f"{nm}{l}"] = np.ascontiguousarray(big[(nm, l)][c * rs:(c + 1) * rs])
        m.update({"x": x[c * T:(c + 1) * T], "pos": pos[c * T:(c + 1) * T].reshape(1, T),
                  "cvec": cvec, "selv": selv, "sel16": sel16, "ident": ident, "masks": masks})
        maps.append(m)
    return maps


def kernel(**inputs):
    nc = build_nc()
    res = run_bass_kernel_spmd(nc, _in_maps(inputs), core_ids=list(range(NCORES)))
    out = np.concatenate([np.asarray(r["out"], dtype=np.float32) for r in res.results], axis=0)
    return out.reshape(1, SEQ, D)
```
